# Optimizing a Trainium2 kernel written in Bass

```python
import math
import jax, jax.numpy as jnp
from jax import lax
import numpy as np

D_MODEL = 1024
BATCH = 8
SEQ = 2048
DEPTH = 1
DEC_BATCH = 32
DEC_SEQ = 1
PAST_LEN = 16384
PAGE_SIZE = 128

RET_HEADS = 8
RET_DK = 64
RET_DV = 128
RET_CHUNK = 128
ROPE_BASE = 10000.0
ATT_WINDOWS = (128, 512, 2048)
ATT_DILATIONS = (1, 4, 16)
ATT_GROUPS = 3
ATT_HEADS_PER_GROUP = 4
ATT_HEADS = ATT_GROUPS * ATT_HEADS_PER_GROUP
ATT_HD = 128
REL_BUCKETS = 32
REL_MAX_DIST = 2048
D_FF = 4 * D_MODEL
D_PLE = 256
NORM_EPS = 1e-6

RET_QK_W = RET_HEADS * RET_DK
RET_V_W = RET_HEADS * RET_DV
ATT_W = ATT_HEADS * ATT_HD
ATT_OUT_W = ATT_HEADS_PER_GROUP * ATT_HD
IN_SPLITS = (RET_QK_W, RET_QK_W, RET_V_W, RET_V_W, ATT_W, ATT_W, ATT_W, D_MODEL, D_MODEL)
N_IN = sum(IN_SPLITS)

kernel_name = "retention_dilated_swa_hybrid_step"


def rms_norm(x, g):
    xf = x.astype(jnp.float32)
    y = xf * lax.rsqrt(jnp.mean(xf * xf, axis=-1, keepdims=True) + NORM_EPS)
    return (y * g.astype(jnp.float32)).astype(x.dtype)


def rotary(x, pos):
    half = x.shape[-1] // 2
    inv = ROPE_BASE ** (-jnp.arange(half, dtype=jnp.float32) / half)
    ang = pos.astype(jnp.float32)[:, None] * inv[None, :]
    cos = jnp.cos(ang)[None, :, None, :]
    sin = jnp.sin(ang)[None, :, None, :]
    x1, x2 = x[..., :half], x[..., half:]
    return jnp.concatenate([x1 * cos - x2 * sin, x1 * sin + x2 * cos], axis=-1)


def ret_log_decay():
    return jnp.log1p(-jnp.exp2(-5.0 - jnp.arange(RET_HEADS, dtype=jnp.float32)))


def retention_chunk(state, q, k, v, log_gamma):
    C = q.shape[1]
    i = jnp.arange(C, dtype=jnp.float32)
    diff = i[:, None] - i[None, :]
    decay = jnp.where(diff[None] >= 0,
                      jnp.exp(jnp.maximum(diff, 0.0)[None] * log_gamma[:, None, None]), 0.0)
    scores = jnp.einsum('bihk,bjhk->bhij', q, k) * decay[None]
    o_in = jnp.einsum('bhij,bjhv->bihv', scores, v)
    q_decay = jnp.exp((i + 1.0)[:, None] * log_gamma[None, :])
    o_cross = jnp.einsum('bihk,bhkv->bihv', q, state) * q_decay[None, :, :, None]
    k_decay = jnp.exp((C - 1.0 - i)[:, None] * log_gamma[None, :])
    new_state = (state * jnp.exp(C * log_gamma)[None, :, None, None]
                 + jnp.einsum('bjhk,bjhv->bhkv', k * k_decay[None, :, :, None], v))
    return new_state, o_in + o_cross


def retention_scan(q, k, v, state0, log_gamma):
    B, S, H, _ = q.shape
    nc = S // RET_CHUNK
    xs = tuple(a.reshape(B, nc, RET_CHUNK, H, a.shape[-1]).swapaxes(0, 1) for a in (q, k, v))

    def step(st, chunk):
        qc, kc, vc = chunk
        return retention_chunk(st, qc, kc, vc, log_gamma)

    st, o = lax.scan(step, state0, xs)
    return o.swapaxes(0, 1).reshape(B, S, H, v.shape[-1]), st


def retention_readout(o, g, gn_g):
    B, T = o.shape[0], o.shape[1]
    mu = jnp.mean(o, axis=-1, keepdims=True)
    var = jnp.mean(jnp.square(o - mu), axis=-1, keepdims=True)
    on = ((o - mu) * lax.rsqrt(var + NORM_EPS)).reshape(B, T, RET_V_W) * gn_g.astype(jnp.float32)
    return jax.nn.silu(g.astype(jnp.float32)) * on


def rel_bucket(dist):
    max_exact = REL_BUCKETS // 2
    d = dist.astype(jnp.int32)
    log_ratio = jnp.log(jnp.maximum(d, 1).astype(jnp.float32) / max_exact) / math.log(REL_MAX_DIST / max_exact)
    large = max_exact + (log_ratio * (REL_BUCKETS - max_exact)).astype(jnp.int32)
    large = jnp.minimum(large, REL_BUCKETS - 1)
    return jnp.where(d < max_exact, d, large)


def group_bias(rel_bias, g):
    dil = ATT_DILATIONS[g]
    nk = ATT_WINDOWS[g] // dil
    b = rel_bucket(jnp.arange(nk) * dil)
    tab = rel_bias[b][:, g * ATT_HEADS_PER_GROUP:(g + 1) * ATT_HEADS_PER_GROUP]
    return tab.T.astype(jnp.float32)


def dilated_attn_prompt(q, k, v, dil, nk, bias):
    B, S, H, E = q.shape
    L = S // dil
    nb = -(-L // nk)
    Lp = nb * nk

    def to_stream(a):
        a = a.reshape(B, L, dil, H, E).transpose(0, 2, 1, 3, 4)
        a = jnp.pad(a, ((0, 0), (0, 0), (0, Lp - L), (0, 0), (0, 0)))
        return a.reshape(B, dil, nb, nk, H, E)

    def with_prev(a):
        prev = jnp.pad(a, ((0, 0), (0, 0), (1, 0), (0, 0), (0, 0), (0, 0)))[:, :, :nb]
        return jnp.concatenate([prev, a], axis=3)

    qs = to_stream(q)
    kk = with_prev(to_stream(k))
    vv = with_prev(to_stream(v))
    i = jnp.arange(nk)[:, None]
    c = jnp.arange(2 * nk)[None, :]
    jd = i + nk - c
    rel_ok = (jd >= 0) & (jd < nk)
    blk = jnp.arange(nb)[:, None, None]
    ok = rel_ok[None] & ((blk > 0) | (c[None] >= nk))
    b_full = bias[:, jnp.clip(jd, 0, nk - 1)]
    s = jnp.einsum('bdnihe,bdnche->bdnhic', qs, kk) * (E ** -0.5) + b_full
    s = jnp.where(ok[:, None], s, -jnp.inf)
    m = jnp.max(s, axis=-1, keepdims=True)
    p = jnp.exp(s - m)
    den = jnp.sum(p, axis=-1, keepdims=True)
    o = jnp.einsum('bdnhic,bdnche->bdnihe', p, vv) / jnp.swapaxes(den, 3, 4)
    lse = jnp.swapaxes((m + jnp.log(den))[..., 0], 3, 4)

    def from_stream(a):
        a = a.reshape((B, dil, Lp) + a.shape[4:])[:, :, :L]
        a = jnp.swapaxes(a, 1, 2)
        return a.reshape((B, S) + a.shape[3:])

    return from_stream(o), from_stream(lse)


def dilated_attn_sample(q, k_new, v_new, k_buf, v_buf, dil, nk, bias, window):
    T = q.shape[1]
    Lb = k_buf.shape[1]
    kf = jnp.concatenate([k_buf.astype(jnp.float32), k_new], axis=1)
    vf = jnp.concatenate([v_buf.astype(jnp.float32), v_new], axis=1)
    idx = Lb + jnp.arange(T)[:, None] - jnp.arange(nk)[None, :] * dil
    ok = idx >= 0
    kg = jnp.take(kf, jnp.maximum(idx, 0), axis=1)
    vg = jnp.take(vf, jnp.maximum(idx, 0), axis=1)
    s = jnp.einsum('bthe,btjhe->bthj', q, kg) * (q.shape[-1] ** -0.5) + bias.T[None, None] .T.T if False else jnp.einsum('bthe,btjhe->bthj', q, kg) * (q.shape[-1] ** -0.5) + bias[None, None]
    s = jnp.where(ok[None, :, None, :], s, -jnp.inf)
    m = jnp.max(s, axis=-1, keepdims=True)
    p = jnp.exp(s - m)
    den = jnp.sum(p, axis=-1, keepdims=True)
    o = jnp.einsum('bthj,btjhe->bthe', p, vg) / den
    lse = (m + jnp.log(den))[..., 0]
    n_keep = min(window, Lb + T)
    return o, lse, kf[:, kf.shape[1] - n_keep:], vf[:, vf.shape[1] - n_keep:]


def decoder_layer(x, ple, pos, ret_state, kv_cache, lw, rel_bias):
    ln1, w_in_l, gn_g, w_rb, w_ab, w_o, ln2, w_u, w_d, w_pl, w_pg = lw
    f32 = jnp.float32
    B, T, _ = x.shape
    h = rms_norm(x, ln1)
    split_idx = [int(s) for s in np.cumsum(IN_SPLITS)[:-1]]
    rq, rk, rv, rg, aq, ak, av, gr, ga = jnp.split(h @ w_in_l, split_idx, axis=-1)
    rq = rotary(rq.reshape(B, T, RET_HEADS, RET_DK).astype(f32), pos)
    rk = rotary(rk.reshape(B, T, RET_HEADS, RET_DK).astype(f32), pos) * (RET_DK ** -0.5)
    rv = rv.reshape(B, T, RET_HEADS, RET_DV).astype(f32)
    lg = ret_log_decay()
    if ret_state is None:
        ro, new_ret = retention_scan(rq, rk, rv, jnp.zeros((B, RET_HEADS, RET_DK, RET_DV), f32), lg)
    else:
        new_ret, ro = retention_chunk(ret_state.astype(f32), rq, rk, rv, lg)
    ret_out = retention_readout(ro, rg, gn_g)
    aq = aq.reshape(B, T, ATT_GROUPS, ATT_HEADS_PER_GROUP, ATT_HD).astype(f32)
    ak = ak.reshape(B, T, ATT_GROUPS, ATT_HEADS_PER_GROUP, ATT_HD).astype(f32)
    av = av.reshape(B, T, ATT_GROUPS, ATT_HEADS_PER_GROUP, ATT_HD).astype(f32)
    outs, lses, new_kv = [], [], []
    for g in range(ATT_GROUPS):
        bias = group_bias(rel_bias, g)
        dil = ATT_DILATIONS[g]
        nk = ATT_WINDOWS[g] // dil
        if kv_cache is None:
            o, lse = dilated_attn_prompt(aq[:, :, g], ak[:, :, g], av[:, :, g], dil, nk, bias)
            keep = min(ATT_WINDOWS[g], T)
            kb, vb = ak[:, T - keep:, g], av[:, T - keep:, g]
        else:
            o, lse, kb, vb = dilated_attn_sample(aq[:, :, g], ak[:, :, g], av[:, :, g],
                                                 kv_cache[g][0], kv_cache[g][1], dil, nk, bias, ATT_WINDOWS[g])
        outs.append(o)
        lses.append(lse)
        new_kv.append((kb.astype(x.dtype), vb.astype(x.dtype)))
    alpha = jax.nn.softmax(jnp.stack(lses, axis=0), axis=0)
    att_out = jnp.sum(alpha[..., None] * jnp.stack(outs, axis=0), axis=0).reshape(B, T, ATT_OUT_W)
    mixed = (jax.nn.sigmoid(gr.astype(f32)) * (ret_out.astype(x.dtype) @ w_rb).astype(f32)
             + jax.nn.sigmoid(ga.astype(f32)) * (att_out.astype(x.dtype) @ w_ab).astype(f32))
    x = x + mixed.astype(x.dtype) @ w_o
    u = rms_norm(x, ln2) @ w_u
    x = x + jnp.square(jax.nn.relu(u)) @ w_d
    x = x + jax.nn.sigmoid(x @ w_pg) * (ple.astype(x.dtype) @ w_pl)
    return x, new_ret.astype(x.dtype), new_kv


def setup_inputs(seed: int = 0) -> dict:
    key = jax.random.key(seed)
    ks = iter(jax.random.split(key, 40))
    f32 = jnp.float32

    def nrm(shape, scale):
        return jax.random.normal(next(ks), shape, f32) * scale

    def gain(shape):
        return 1.0 + 0.05 * jax.random.normal(next(ks), shape, f32)

    out = {
        "x_prompt": nrm((BATCH, SEQ, D_MODEL), 1.0),
        "x_sample": nrm((DEC_BATCH, DEC_SEQ, D_MODEL), 1.0),
        "state_ret": nrm((DEPTH, DEC_BATCH, RET_HEADS, RET_DK, RET_DV), 0.5),
    }
    for w in ATT_WINDOWS:
        lb = min(w, PAST_LEN)
        out["cache_k_w%d" % w] = nrm((DEPTH, DEC_BATCH, lb, ATT_HEADS_PER_GROUP, ATT_HD), 1.0)
        out["cache_v_w%d" % w] = nrm((DEPTH, DEC_BATCH, lb, ATT_HEADS_PER_GROUP, ATT_HD), 1.0)
    out.update({
        "p_prompt": nrm((DEPTH, BATCH, SEQ, D_PLE), 1.0),
        "p_sample": nrm((DEPTH, DEC_BATCH, DEC_SEQ, D_PLE), 1.0),
        "ln1_g": gain((DEPTH, D_MODEL)),
        "w_in": nrm((DEPTH, D_MODEL, N_IN), D_MODEL ** -0.5),
        "ret_gn_g": gain((DEPTH, RET_V_W)),
        "w_ret_br": nrm((DEPTH, RET_V_W, D_MODEL), RET_V_W ** -0.5),
        "w_att_br": nrm((DEPTH, ATT_OUT_W, D_MODEL), ATT_OUT_W ** -0.5),
        "w_out": nrm((DEPTH, D_MODEL, D_MODEL), D_MODEL ** -0.5),
        "ln2_g": gain((DEPTH, D_MODEL)),
        "w_up": nrm((DEPTH, D_MODEL, D_FF), D_MODEL ** -0.5),
        "w_down": nrm((DEPTH, D_FF, D_MODEL), D_FF ** -0.5),
        "w_ple": nrm((DEPTH, D_PLE, D_MODEL), D_PLE ** -0.5),
        "w_ple_gate": nrm((DEPTH, D_MODEL, D_MODEL), D_MODEL ** -0.5),
        "rel_bias": nrm((REL_BUCKETS, ATT_HEADS), 0.5),
        "lnf_g": gain((D_MODEL,)),
    })
    return out


def reference(x_prompt, x_sample, state_ret, cache_k_w128, cache_v_w128, cache_k_w512, cache_v_w512,
              cache_k_w2048, cache_v_w2048, p_prompt, p_sample, ln1_g, w_in, ret_gn_g, w_ret_br, w_att_br,
              w_out, ln2_g, w_up, w_down, w_ple, w_ple_gate, rel_bias, lnf_g):
    cache_k = (cache_k_w128, cache_k_w512, cache_k_w2048)
    cache_v = (cache_v_w128, cache_v_w512, cache_v_w2048)
    xp, xs = x_prompt, x_sample
    pos_p = jnp.arange(xp.shape[1], dtype=jnp.int32)
    pos_s = PAST_LEN + jnp.arange(xs.shape[1], dtype=jnp.int32)
    ret_p, ret_s = [], []
    kv_p = [([], []) for _ in range(ATT_GROUPS)]
    kv_s = [([], []) for _ in range(ATT_GROUPS)]
    for l in range(DEPTH):
        lw = (ln1_g[l], w_in[l], ret_gn_g[l], w_ret_br[l], w_att_br[l], w_out[l], ln2_g[l],
              w_up[l], w_down[l], w_ple[l], w_ple_gate[l])
        xp, st_p, nkv_p = decoder_layer(xp, p_prompt[l], pos_p, None, None, lw, rel_bias)
        layer_cache = [(cache_k[g][l], cache_v[g][l]) for g in range(ATT_GROUPS)]
        xs, st_s, nkv_s = decoder_layer(xs, p_sample[l], pos_s, state_ret[l], layer_cache, lw, rel_bias)
        ret_p.append(st_p)
        ret_s.append(st_s)
        for g in range(ATT_GROUPS):
            kv_p[g][0].append(nkv_p[g][0])
            kv_p[g][1].append(nkv_p[g][1])
            kv_s[g][0].append(nkv_s[g][0])
            kv_s[g][1].append(nkv_s[g][1])
    y_prompt = rms_norm(xp, lnf_g)
    y_sample = rms_norm(xs, lnf_g)
    new_state_ret_prompt = jnp.stack(ret_p)
    new_k_w128_prompt = jnp.stack(kv_p[0][0])
    new_v_w128_prompt = jnp.stack(kv_p[0][1])
    new_k_w512_prompt = jnp.stack(kv_p[1][0])
    new_v_w512_prompt = jnp.stack(kv_p[1][1])
    new_k_w2048_prompt = jnp.stack(kv_p[2][0])
    new_v_w2048_prompt = jnp.stack(kv_p[2][1])
    new_state_ret_sample = jnp.stack(ret_s)
    new_k_w128_sample = jnp.stack(kv_s[0][0])
    new_v_w128_sample = jnp.stack(kv_s[0][1])
    new_k_w512_sample = jnp.stack(kv_s[1][0])
    new_v_w512_sample = jnp.stack(kv_s[1][1])
    new_k_w2048_sample = jnp.stack(kv_s[2][0])
    new_v_w2048_sample = jnp.stack(kv_s[2][1])
    return (y_prompt, y_sample,
            new_state_ret_prompt, new_k_w128_prompt, new_v_w128_prompt, new_k_w512_prompt, new_v_w512_prompt,
            new_k_w2048_prompt, new_v_w2048_prompt,
            new_state_ret_sample, new_k_w128_sample, new_v_w128_sample, new_k_w512_sample, new_v_w512_sample,
            new_k_w2048_sample, new_v_w2048_sample)
```

```python
import math
import numpy as np
import ml_dtypes
import concourse.bass as bass
import concourse.mybir as mybir
from concourse.bass_utils import run_bass_kernel_spmd

F32 = mybir.dt.float32
BF16 = mybir.dt.bfloat16
AF = mybir.ActivationFunctionType
ALU = mybir.AluOpType
AX = mybir.AxisListType

S = 2048
NT = 2052
D = 1024
EPS = 1e-6
WINS = (128, 512, 2048)
DILS = (1, 4, 16)
NEG = -30000.0
SAME_ENG_SYNC = True
TB = [(0, 512), (512, 512), (1024, 512), (1536, 512), (2048, 4)]

CF = {}
_o = 0
for _n, _w in (("identf", 128), ("onesf", 128), ("cos", 17 * 32), ("sin", 17 * 32), ("qdec", 8), ("kdec", 8),
               ("qdec_s", 8), ("kdec_s", 8), ("gamC", 4), ("bd", 4), ("sel", 512), ("eps", 1)):
    CF[_n] = (_o, _w)
    _o += _w
NCF = _o
NCB = 384


class Ev:
    __slots__ = ("key", "val", "eng")

    def __init__(self, key, val, eng):
        self.key = key
        self.val = val
        self.eng = eng


class Tile:
    __slots__ = ("name", "w", "r")

    def __init__(self, name):
        self.name = name
        self.w = None
        self.r = {}


def handover(olds, news):
    evs = {}
    for t in olds:
        if t.w is not None:
            evs[("w", t.name)] = t.w
        for k, e in t.r.items():
            evs[(k, t.name)] = e
    for t in news:
        t.r.update(evs)


class Prog:
    def __init__(self, nc):
        self.nc = nc
        self.sems = {}
        self.semval = {}

    def sem(self, key):
        if key not in self.sems:
            self.sems[key] = self.nc.semaphore(key).__enter__()
            self.semval[key] = 0
        return self.sems[key]


class Eng:
    def __init__(self, prog, name, h):
        self.P = prog
        self.name = name
        self.h = h
        self.key = "e_" + name
        prog.sem(self.key)
        self.count = 0
        self.known = {}
        self.pending = []

    def _wait(self, evs):
        need = {}
        for e in evs:
            if e is None:
                continue
            if e.eng is self and (self.name == "pe" or not SAME_ENG_SYNC):
                continue
            if e.val is None:
                raise RuntimeError("dependency on an unsignaled op (%s)" % e.key)
            if need.get(e.key, 0) < e.val:
                need[e.key] = e.val
        for k, v in need.items():
            if self.known.get(k, 0) >= v:
                continue
            self.h.wait_ge(self.P.sems[k], v)
            self.known[k] = v

    @staticmethod
    def _deps(reads, writes):
        evs = []
        for t in reads:
            evs.append(t.w)
        for t in writes:
            evs.append(t.w)
            evs.extend(t.r.values())
        return evs

    def op(self, fn, reads=(), writes=(), signal=True):
        self._wait(self._deps(reads, writes))
        ins = fn(self.h)
        ev = Ev(self.key, None, self)
        if signal:
            self.count += 1
            ins.then_inc(self.P.sems[self.key], 1)
            ev.val = self.count
            for p in self.pending:
                p.val = self.count
            self.pending = []
        else:
            self.pending.append(ev)
        for t in reads:
            t.r[self.name] = ev
        for t in writes:
            t.w = ev
            t.r = {}
        return ev

    def dma(self, out, in_, semkey, reads=(), writes=()):
        self._wait(self._deps(reads, writes))
        sem = self.P.sem(semkey)
        self.P.semval[semkey] += 16
        self.h.dma_start(out=out, in_=in_).then_inc(sem, 16)
        ev = Ev(semkey, self.P.semval[semkey], None)
        for t in reads:
            t.r["dma_" + semkey] = ev
        for t in writes:
            t.w = ev
            t.r = {}
        return ev


def blk_ap(ap2d, d, blk):
    if d == 1:
        return ap2d[:, blk * 512:(blk + 1) * 512]
    if d == 4:
        return ap2d[:, blk:2048:4]
    return ap2d[:, 0:2048].rearrange("p (l r) -> p r l", r=16)[:, 4 * blk:4 * blk + 4, :]


def tile_ap(ap2d, d, t):
    if d == 1:
        return ap2d[:, t * 128:(t + 1) * 128]
    if d == 4:
        r, a = t // 4, t % 4
        return ap2d[:, r + 512 * a:r + 512 * a + 512:4]
    return ap2d[:, t:2048:16]


def as_blk_shape(ap2d_512, d):
    if d == 16:
        return ap2d_512.rearrange("p (r l) -> p r l", r=4)
    return ap2d_512


def build_program(dbg=False, stop=None):
    nc = bass.Bass("TRN2", target_bir_lowering=False)
    P = Prog(nc)

    def din(name, shape, dt=F32):
        return nc.dram_tensor(name, list(shape), dt, kind="ExternalInput").ap()

    def dout(name, shape, dt=F32):
        return nc.dram_tensor(name, list(shape), dt, kind="ExternalOutput").ap()

    x = din("x", [S, D])
    xs = din("xs", [4, D])
    st_in = din("st", [4, 8, 64, 128])
    ck = [din("ck%d" % g, [4, WINS[g], 512]) for g in range(3)]
    cv = [din("cv%d" % g, [4, WINS[g], 512]) for g in range(3)]
    pp = din("pp", [S, 256])
    psm = din("psm", [4, 256])
    ln1g = din("ln1g", [1, D])
    w_in = din("w_in", [D, 9728])
    gng = din("gng", [1, D])
    w_rb = din("w_rb", [D, D])
    w_ab = din("w_ab", [512, D])
    w_o = din("w_o", [D, D])
    ln2g = din("ln2g", [D])
    w_up = din("w_up", [D, 4096])
    w_dn = din("w_dn", [4096, D])
    w_pl = din("w_pl", [256, D])
    w_pg = din("w_pg", [D, D])
    relb = din("relb", [32, 12])
    lnfg = din("lnfg", [1, D])
    cf_d = din("cf", [128, NCF])
    cb_d = din("cb", [128, NCB], BF16)
    ohg_d = din("ohg", [33, 3 * 383])
    ohs_d = din("ohs", [33, 3 * 128])
    y = dout("y", [S, D])
    ys = dout("ys", [4, D])
    stp = dout("stp", [8, 64, 128])
    kp = [dout("kp%d" % g, [WINS[g], 512]) for g in range(3)]
    vp = [dout("vp%d" % g, [WINS[g], 512]) for g in range(3)]
    sts = dout("sts", [4, 8, 64, 128])
    ks = [dout("ks%d" % g, [4, WINS[g], 512]) for g in range(3)]
    vs = [dout("vs%d" % g, [4, WINS[g], 512]) for g in range(3)]
    tvec = nc.dram_tensor("tvec", [12, 383], F32, kind="Internal").ap()
    tskew = nc.dram_tensor("tskew", [12, 128 * 383], F32, kind="Internal").ap()
    tvec_t = Tile("tvec")
    tskew_ts = [Tile("tskew%d" % i) for i in range(12)]

    pe = Eng(P, "pe", nc.tensor)
    act = Eng(P, "act", nc.scalar)
    dve = Eng(P, "dve", nc.vector)
    pool = Eng(P, "pool", nc.gpsimd)
    sp = Eng(P, "sp", nc.sync)

    def sb(name, shape, dt):
        return nc.sbuf_tensor(name, list(shape), dt).__enter__()

    R1 = sb("R1", [128, 8 * NT], BF16)
    R2 = sb("R2", [128, 4 * NT], F32)
    R4 = sb("R4", [128, 4 * NT], F32)
    R3 = sb("R3", [128, 4 * NT], BF16)
    R5 = sb("R5", [128, 24576], BF16)
    WS = sb("WS", [128, 3, 4096], BF16)
    cf = sb("cf_sb", [128, NCF], F32)
    cb = sb("cb_sb", [128, NCB], BF16)
    gt = sb("gt", [128, D], F32)
    bsl = sb("bsl", [128, 2, 256], F32)
    sm = sb("sm", [128, 256], F32)
    ps = nc.psum_tensor("psum_all", [128, 8, 512], F32).__enter__()

    cf_t, cb_t, gt_t = Tile("cf"), Tile("cb"), Tile("gt")
    bank_t = [Tile("bank%d" % i) for i in range(8)]
    ws_t = [Tile("ws%d" % i) for i in range(3)]
    bsl_t = [Tile("bsl0"), Tile("bsl1")]

    def bank(i):
        return ps[:, i, :]

    def bankbf(i):
        return ps[:, i, :].bitcast(BF16)

    def cfa(name, rows=128):
        o, w = CF[name]
        return cf[0:rows, o:o + w]

    identb = cb[:, 0:128]
    onesb = cb[:, 128:256]
    causal = cb[:, 256:384]
    identf = cfa("identf")
    onesf = cfa("onesf")

    hT = R1[:, :].rearrange("p (k n) -> p k n", k=8)
    hT_t = Tile("hT")
    attT = R3[:, :].rearrange("p (k n) -> p k n", k=4)
    attT_t = Tile("attT")
    R2b = R2[:, :].bitcast(BF16)
    retT = R2b.rearrange("p (k n) -> p k n", k=8)
    retT_t = Tile("retT")
    R5f = R5[:, :].bitcast(F32)

    sp.dma(cf[:, :], cf_d[:, :], "c_cf", writes=[cf_t])
    sp.dma(cb[:, :], cb_d[:, :], "c_cb", writes=[cb_t])

    ws_state = {"n": 0}

    def load_w(src, k_rows, col0, ncols, kchunks, slot=None):
        if slot is None:
            slot = ws_state["n"] % 3
            ws_state["n"] += 1
        view = WS[:, slot, 0:kchunks * ncols].rearrange("p (k n) -> p k n", k=kchunks)
        srcv = src[k_rows:k_rows + kchunks * 128, col0:col0 + ncols].rearrange("(k p) n -> p k n", p=128)
        pool.dma(view, srcv, "ws%d" % slot, writes=[ws_t[slot]])
        return slot, view

    if stop == 'bias':
        return nc, P, sp, locals()
    xt_sl = [R5f[:, 4096 + 1024 * i:5120 + 1024 * i] for i in range(4)]
    xt_t = [Tile("xt%d" % i) for i in range(4)]
    hbs = [R5[:, 0:1024], R5[:, 2048:3072]]
    hb_ts = [Tile("hb0"), Tile("hb1")]
    hb, hb_t = hbs[0], hb_ts[0]
    junks = [R5[:, 1024:2048], R5[:, 3072:4096]]
    junk_ts = [Tile("junk0"), Tile("junk1")]
    junk, junk_t = junks[0], junk_ts[0]
    st_t = Tile("stats")
    st0_t = [Tile("st0a"), Tile("st0b")]
    sp.dma(gt[:, :], ln1g[0:1, :].broadcast_to([128, D]), "c_gt", writes=[gt_t])
    epsc = cfa("eps")

    wq_s, wq_v = load_w(w_in, 0, 3072 + 0, 512, 8)
    wk_s, wk_v = load_w(w_in, 0, 4608 + 0, 512, 8)
    wv_s, wv_v = load_w(w_in, 0, 6144 + 0, 512, 8)
    for t in range(4):
        sp.dma(xt_sl[t][:, :], x[t * 128:(t + 1) * 128, :], "xt%d" % t, writes=[xt_t[t]])
    rbx = sm[0:33, 0:12]
    rbx_t = Tile("rbx")
    oh_t = Tile("oh")
    ohg = R4[0:33, 0:3 * 383]
    ohs = R4[0:33, 1152:1152 + 384]
    dve.op(lambda e: e.memset(sm[0:64, 0:12], 1.0), writes=[rbx_t])
    sp.dma(sm[0:32, 0:12], relb[:, :], "c_rb", writes=[rbx_t])
    sp.dma(ohg, ohg_d[:, :], "c_oh", writes=[oh_t])
    sp.dma(ohs, ohs_d[:, :], "c_oh", writes=[oh_t])
    tv_sb = R4[0:4, 2048:2048 + 3 * 383]
    tv_t = Tile("tv_sb")
    bs_sb = sm[:, 16:28]
    bs_t = Tile("bs_sb")
    for g in range(3):
        pe.op(lambda e, g=g: e.matmul(ps[0:4, 0, 0:383], lhsT=rbx[:, 4 * g:4 * g + 4], rhs=ohg[:, 383 * g:383 * (g + 1)],
                                      start=True, stop=True),
              reads=[rbx_t, oh_t], writes=[bank_t[0]])
        act.op(lambda e, g=g: e.copy(out=tv_sb[:, 383 * g:383 * (g + 1)], in_=ps[0:4, 0, 0:383]),
               reads=[bank_t[0]], writes=[tv_t])
        pe.op(lambda e, g=g: e.matmul(ps[:, 1, 0:4], lhsT=ohs[:, 128 * g:128 * (g + 1)], rhs=rbx[:, 4 * g:4 * g + 4],
                                      start=True, stop=True),
              reads=[rbx_t, oh_t], writes=[bank_t[1]])
        act.op(lambda e, g=g: e.copy(out=bs_sb[:, 4 * g:4 * g + 4], in_=ps[:, 1, 0:4]),
               reads=[bank_t[1]], writes=[bs_t])
    for g in range(3):
        pool.dma(tvec[4 * g:4 * g + 4, :], tv_sb[:, 383 * g:383 * (g + 1)], "c_tv", reads=[tv_t], writes=[tvec_t])
    for gh in range(12):
        pool.dma(tskew[gh, :].rearrange("(c n) -> c n", n=383), tvec[gh:gh + 1, :].broadcast_to([128, 383]),
                 "c_ts%d" % gh, reads=[tvec_t], writes=[tskew_ts[gh]])

    def load_bias(gh, slot):
        src = tskew[gh, 127:127 + 128 * 382].rearrange("(c n) -> c n", n=382)[:, 0:256]
        sp.dma(bsl[:, slot, :], src, "bsl%d" % slot, reads=[tskew_ts[gh]], writes=[bsl_t[slot]])

    def p0_stage1(t):
        rows = 128 if t < 16 else 4
        sl = t % 2
        col = 32 + 4 * sl
        stt_ = st0_t[sl]
        src = x[t * 128:(t + 1) * 128, :] if t < 16 else xs[:, :]
        x4 = t % 4
        if t >= 4:
            sp.dma(xt_sl[x4][0:rows, :], src, "xt%d" % x4, writes=[xt_t[x4]])
        act.op(lambda e: e.activation(out=junks[sl][0:rows, :], in_=xt_sl[x4][0:rows, :], func=AF.Square,
                                      accum_out=sm[0:rows, col:col + 1]),
               reads=[xt_t[x4]], writes=[junk_ts[sl], stt_])
        act.op(lambda e: e.activation(out=sm[0:rows, col + 1:col + 2], in_=sm[0:rows, col:col + 1], func=AF.Sqrt,
                                      bias=epsc[0:rows, :], scale=1.0 / D),
               reads=[stt_, cf_t], writes=[stt_])
        dve.op(lambda e: e.reciprocal(out=sm[0:rows, col + 2:col + 3], in_=sm[0:rows, col + 1:col + 2]),
               reads=[stt_], writes=[stt_])
        dve.op(lambda e: e.scalar_tensor_tensor(
            out=hbs[sl][0:rows, :], in0=xt_sl[x4][0:rows, :], scalar=sm[0:rows, col + 2:col + 3], in1=gt[0:rows, :],
            op0=ALU.mult, op1=ALU.mult), reads=[xt_t[x4], stt_, gt_t], writes=[hb_ts[sl]])

    def p0_stage2(t):
        rows = 128 if t < 16 else 4
        sl = t % 2
        bi = t % 2
        for k in range(8):
            pe.op(lambda e, k=k: e.transpose(
                out=bankbf(bi)[:, k * 128:k * 128 + rows], in_=hbs[sl][0:rows, k * 128:(k + 1) * 128],
                identity=identb[0:rows, 0:rows]),
                reads=[hb_ts[sl], cb_t], writes=[bank_t[bi]], signal=(k == 7))
        act.op(lambda e: e.copy(
            out=hT[:, :, t * 128:t * 128 + rows],
            in_=bankbf(bi).rearrange("p (k c) -> p k c", k=8)[:, :, 0:rows]),
            reads=[bank_t[bi]], writes=[hT_t])

    p0_stage1(0)
    for t in range(17):
        if t + 1 < 17:
            p0_stage1(t + 1)
        p0_stage2(t)

    if dbg:
        dbg_hT = dout("dbg_hT", [128, 8 * NT], BF16)
        sp.dma(dbg_hT[:, :], R1[:, :], "dbg", reads=[hT_t])

    if stop == 'p0':
        return nc, P, sp, locals()
    numacc = R2[:, :].rearrange("p (h n) -> p h n", h=4)
    denacc = R4[:, :].rearrange("p (h n) -> p h n", h=4)
    num_t = [Tile("num%d" % h) for h in range(4)]
    den_t = [Tile("den%d" % h) for h in range(4)]
    vg = R5[:, 0:8192].rearrange("p (t c) -> p t c", t=16)
    vg_t = Tile("vg")
    qk_sl = [R5[:, 8192 + i * 4096:8192 + (i + 1) * 4096] for i in range(2)]
    qT_t = [Tile("qT0"), Tile("qT1")]
    kT_t = [Tile("kT0"), Tile("kT1")]
    ssb = [R5f[:, 8192 + i * 256:8192 + (i + 1) * 256] for i in range(4)]
    ssb_t = [Tile("ssb%d" % i) for i in range(4)]
    ptb = [R5[:, 18432 + i * 256:18432 + (i + 1) * 256] for i in range(4)]
    ptb_t = [Tile("pt%d" % i) for i in range(4)]
    stg = [R5f[:, 9728 + i * 512:9728 + (i + 1) * 512] for i in range(2)]
    stg_t = [Tile("stg%d" % i) for i in range(2)] + [Tile("stg_unused")]
    ckt = [R5f[:, 10752 + i * 512:10752 + (i + 1) * 512] for i in range(2)]
    ckt_t = [Tile("ckt0"), Tile("ckt1")]
    qs_sb = R5f[0:4, 11776:12288]
    qs_t = Tile("qs_sb")
    nums = sm[:, 40:56]
    dens = sm[:, 56:72]
    nds_t = Tile("nds")
    scale = 128.0 ** -0.5
    handover(hb_ts + junk_ts + xt_t, [vg_t, qT_t[0], qT_t[1], kT_t[0], kT_t[1]])
    sslot_t = [Tile("sslot%d" % i) for i in range(4)]
    handover([bank_t[0], bank_t[1]], sslot_t)

    def sslot(i):
        return ps[:, i % 2, 0:256]
    stg_n = {"n": 0}
    blkn = {"n": 0}

    def issue_d2d(g, b):
        W = WINS[g]
        for dst_, src_ in ((ks[g], ck[g]), (vs[g], cv[g])):
            act.dma(dst_[b, 0:W - 1, :].rearrange("w c -> (w c)").rearrange("(a n) -> a n", a=16),
                    src_[b, 1:W, :].rearrange("w c -> (w c)").rearrange("(a n) -> a n", a=16), "d2d")
    d2d_left = []

    load_bias(0, 0)

    def tm_tile(g, which, t, wslot, wview):
        d = DILS[g]
        dst = vp[g] if which == "v" else kp[g]
        if g == 0:
            need, drows = (t == 15), dst[0:128, :]
        elif g == 1:
            need, drows = (t % 4 == 3), dst[(t // 4):512:4, :]
        else:
            need, drows = True, dst[t:2048:16, :]
        if which == "k" and not need:
            return
        bi = 6 + (blkn["n"] % 2)
        blkn["n"] += 1
        for k in range(8):
            pe.op(lambda e, k=k: e.matmul(bank(bi), lhsT=tile_ap(hT[:, k, :], d, t), rhs=wview[:, k, :],
                                          start=(k == 0), stop=(k == 7)),
                  reads=[hT_t, ws_t[wslot]], writes=[bank_t[bi]], signal=(k == 7))
        if which == "v":
            act.op(lambda e: e.copy(out=vg[:, t, :], in_=bank(bi)), reads=[bank_t[bi]], writes=[vg_t])
        if need:
            si = stg_n["n"] % 2
            stg_n["n"] += 1
            act.op(lambda e: e.copy(out=stg[si], in_=bank(bi)), reads=[bank_t[bi]], writes=[stg_t[si]])
            sp.dma(drows, stg[si], "stg%d" % si, reads=[stg_t[si]])

    def proj_group(g, h, which, blk, wslot, wview):
        d = DILS[g]
        qs_i = h % 2
        dstT = qk_sl[qs_i][:, 0:2048] if which == "q" else qk_sl[qs_i][:, 2048:4096]
        dst_t = qT_t[qs_i] if which == "q" else kT_t[qs_i]
        bi = 6 + (blkn["n"] % 2)
        blkn["n"] += 1
        for k in range(8):
            pe.op(lambda e, k=k: e.matmul(bank(bi), lhsT=wview[:, k, 128 * h:128 * (h + 1)],
                                          rhs=hT[:, k, 512 * blk:512 * (blk + 1)], start=(k == 0), stop=(k == 7)),
                  reads=[hT_t, ws_t[wslot]], writes=[bank_t[bi]], signal=(k == 7))
        if d == 1:
            o_ap, i_ap = dstT[:, 512 * blk:512 * (blk + 1)], bank(bi)
        else:
            lw = 512 // d
            o_ap = dstT.rearrange("p (r l) -> p r l", r=d)[:, :, lw * blk:lw * (blk + 1)]
            i_ap = bank(bi).rearrange("p (l r) -> p r l", r=d)
        act.op(lambda e: e.copy(out=o_ap, in_=i_ap), reads=[bank_t[bi]], writes=[dst_t])

    def core(g, h, filler):
        d = DILS[g]
        L = S // d
        nb = L // 128
        gh = 4 * g + h
        qs_i = h % 2
        qT = qk_sl[qs_i][:, 0:2048]
        kT = qk_sl[qs_i][:, 2048:4096]
        bslot = gh % 2
        if gh + 1 < 12:
            load_bias(gh + 1, (gh + 1) % 2)
        Bt = bsl[:, bslot, :]
        KB = [(r, m) for r in range(d) for m in range(nb)]
        LA = 1

        def emit_S(i):
            r, m = KB[i]
            kb = r * L + m * 128
            nq = 256 if m + 1 < nb else 128
            sl4 = i % 4
            sl2 = i % 2
            pe.op(lambda e: e.matmul(sslot(sl2)[:, 0:nq], lhsT=kT[:, kb:kb + 128], rhs=qT[:, kb:kb + nq], start=True, stop=True),
                  reads=[qT_t[qs_i], kT_t[qs_i]], writes=[sslot_t[sl2]])
            dve.op(lambda e: e.scalar_tensor_tensor(out=ssb[sl4][:, 0:nq], in0=sslot(sl2)[:, 0:nq], scalar=scale,
                                                    in1=Bt[:, 0:nq], op0=ALU.mult, op1=ALU.add),
                   reads=[sslot_t[sl2], bsl_t[bslot]], writes=[ssb_t[sl4]])
            act.op(lambda e: e.activation(out=ptb[sl4][:, 0:nq], in_=ssb[sl4][:, 0:nq], func=AF.Exp),
                   reads=[ssb_t[sl4]], writes=[ptb_t[sl4]])

        def emit_PV(i):
            r, m = KB[i]
            kb = r * L + m * 128
            sl4 = i % 4
            qb = kb // 128
            vt = vg[:, qb, 128 * h:128 * (h + 1)]
            ob = 2 + ((qb // 4) % 2)
            dbk = 4 + ((qb // 4) % 2)
            c0 = (qb % 4) * 128
            pe.op(lambda e: e.matmul(ps[:, ob, c0:c0 + 128], lhsT=vt, rhs=ptb[sl4][:, 0:128], start=(m == 0), stop=True),
                  reads=[vg_t, ptb_t[sl4]], writes=[bank_t[ob]], signal=False)
            pe.op(lambda e: e.matmul(ps[:, dbk, c0:c0 + 128], lhsT=onesb, rhs=ptb[sl4][:, 0:128], start=(m == 0), stop=True),
                  reads=[cb_t, ptb_t[sl4]], writes=[bank_t[dbk]], signal=True)
            if (qb % 4) == 3:
                blk = qb // 4
                nv = blk_ap(numacc[:, h, :], d, blk)
                dv = blk_ap(denacc[:, h, :], d, blk)
                if g == 0:
                    act.op(lambda e: e.copy(out=nv, in_=as_blk_shape(bank(ob), d)), reads=[bank_t[ob]], writes=[num_t[h]])
                    act.op(lambda e: e.copy(out=dv, in_=as_blk_shape(bank(dbk), d)), reads=[bank_t[dbk]], writes=[den_t[h]])
                else:
                    dve.op(lambda e: e.tensor_tensor(out=nv, in0=as_blk_shape(bank(ob), d), in1=nv, op=ALU.add),
                           reads=[bank_t[ob], num_t[h]], writes=[num_t[h]])
                    dve.op(lambda e: e.tensor_tensor(out=dv, in0=as_blk_shape(bank(dbk), d), in1=dv, op=ALU.add),
                           reads=[bank_t[dbk], den_t[h]], writes=[den_t[h]])
            if m + 1 < nb:
                qb2 = qb + 1
                ob2 = 2 + ((qb2 // 4) % 2)
                dbk2 = 4 + ((qb2 // 4) % 2)
                c2 = (qb2 % 4) * 128
                pe.op(lambda e: e.matmul(ps[:, ob2, c2:c2 + 128], lhsT=vt, rhs=ptb[sl4][:, 128:256], start=True, stop=False),
                      reads=[vg_t, ptb_t[sl4]], writes=[bank_t[ob2]], signal=False)
                pe.op(lambda e: e.matmul(ps[:, dbk2, c2:c2 + 128], lhsT=onesb, rhs=ptb[sl4][:, 128:256], start=True, stop=False),
                      reads=[cb_t, ptb_t[sl4]], writes=[bank_t[dbk2]], signal=True)

        for i in range(LA):
            emit_S(i)
        for i in range(16):
            if i + LA < 16:
                emit_S(i + LA)
            if i % 2 == 1 and filler:
                filler.pop(0)()
            emit_PV(i)
        while filler:
            filler.pop(0)()

    for g in range(3):
        d = DILS[g]
        W = WINS[g]
        tm_work = [(lambda t=t, a=wv_s, b_=wv_v: tm_tile(g, "v", t, a, b_)) for t in range(16)]
        if g < 2:
            tm_work += [(lambda t=t, a=wk_s, b_=wk_v: tm_tile(g, "k", t, a, b_)) for t in range(16)]
        tm_per_b = (len(tm_work) + 3) // 4
        svals = {}
        for wi, (wslot, wview) in enumerate(((wq_s, wq_v), (wk_s, wk_v), (wv_s, wv_v))):
            bi = 6 + (blkn["n"] % 2)
            blkn["n"] += 1
            for k in range(8):
                pe.op(lambda e, k=k, bi=bi, wview=wview: e.matmul(
                    ps[0:4, bi, :], lhsT=hT[:, k, 2048:2052], rhs=wview[:, k, :], start=(k == 0), stop=(k == 7)),
                    reads=[hT_t, ws_t[wslot]], writes=[bank_t[bi]], signal=(k == 7))
            if wi == 0:
                act.op(lambda e, bi=bi: e.copy(out=qs_sb, in_=ps[0:4, bi, :]), reads=[bank_t[bi]], writes=[qs_t])
            else:
                nrow = gt[0:4, 512 * (wi - 1):512 * wi]
                act.op(lambda e, bi=bi, nrow=nrow: e.copy(out=nrow, in_=ps[0:4, bi, :]),
                       reads=[bank_t[bi]], writes=[gt_t])
                dst = ks[g] if wi == 1 else vs[g]
                sp.dma(dst[:, W - 1, :], nrow, "c_nr", reads=[gt_t])
                svals[wi] = nrow
        for b in range(4):
            for wi, cache in ((1, ck[g]), (2, cv[g])):
                ci = wi - 1
                sp.dma(ckt[ci][:, :], cache[b, 0:W:d, :], "ckt%d" % ci, writes=[ckt_t[ci]])
                sp.dma(ckt[ci][0:1, :], svals[wi][b:b + 1, :], "ckt%d" % ci, reads=[gt_t],
                       writes=[ckt_t[ci]])
            bq = 6 + (blkn["n"] % 2)
            blkn["n"] += 1
            o_sel = CF["sel"][0]
            pe.op(lambda e, b=b, bq=bq: e.matmul(bank(bq), lhsT=cf[0:4, o_sel + 128 * b:o_sel + 128 * (b + 1)],
                                                 rhs=qs_sb, start=True, stop=True),
                  reads=[cf_t, qs_t], writes=[bank_t[bq]])
            prodv = R5f[:, 8192:8704]
            dve.op(lambda e, bq=bq: e.tensor_tensor(out=prodv, in0=bank(bq), in1=ckt[0][:, :], op=ALU.mult),
                   reads=[bank_t[bq], ckt_t[0]], writes=[ssb_t[0], ssb_t[1]])
            dve.op(lambda e: e.tensor_reduce(out=sm[:, 80:84], in_=prodv.rearrange("p (h e) -> p h e", h=4),
                                             axis=AX.X, op=ALU.add),
                   reads=[ssb_t[0], ssb_t[1]], writes=[st_t])
            dve.op(lambda e, g=g: e.scalar_tensor_tensor(out=sm[:, 84:88], in0=sm[:, 80:84], scalar=scale,
                                                         in1=bs_sb[:, 4 * g:4 * g + 4], op0=ALU.mult, op1=ALU.add),
                   reads=[st_t, bs_t], writes=[st_t])
            act.op(lambda e: e.activation(out=sm[:, 88:92], in_=sm[:, 84:88], func=AF.Exp),
                   reads=[st_t], writes=[st_t])
            for _ in range(tm_per_b):
                if tm_work:
                    tm_work.pop(0)()
            bo = 4 + (b % 2)
            for h in range(4):
                pe.op(lambda e, h=h, bo=bo: e.matmul(ps[:, bo, 8 + 4 * h:12 + 4 * h], lhsT=ckt[1][:, 128 * h:128 * (h + 1)],
                                                     rhs=sm[:, 88:92], start=True, stop=True),
                      reads=[ckt_t[1], st_t], writes=[bank_t[bo]], signal=False)
            pe.op(lambda e, bo=bo: e.matmul(ps[:, bo, 4:8], lhsT=onesf, rhs=sm[:, 88:92], start=True, stop=True),
                  reads=[cf_t, st_t], writes=[bank_t[bo]])
            if g == 0:
                act.op(lambda e, b=b, bo=bo: e.copy(out=nums[:, 4 * b:4 * b + 4], in_=ps[:, bo, 8:24:5]),
                       reads=[bank_t[bo]], writes=[nds_t])
                act.op(lambda e, b=b, bo=bo: e.copy(out=dens[:, 4 * b:4 * b + 4], in_=ps[:, bo, 4:8]),
                       reads=[bank_t[bo]], writes=[nds_t])
            else:
                dve.op(lambda e, b=b, bo=bo: e.tensor_tensor(out=nums[:, 4 * b:4 * b + 4], in0=ps[:, bo, 8:24:5],
                                                             in1=nums[:, 4 * b:4 * b + 4], op=ALU.add),
                       reads=[bank_t[bo], nds_t], writes=[nds_t])
                dve.op(lambda e, b=b, bo=bo: e.tensor_tensor(out=dens[:, 4 * b:4 * b + 4], in0=ps[:, bo, 4:8],
                                                             in1=dens[:, 4 * b:4 * b + 4], op=ALU.add),
                       reads=[bank_t[bo], nds_t], writes=[nds_t])

        while tm_work:
            tm_work.pop(0)()
        if g + 1 < 3:
            wv_s, wv_v = load_w(w_in, 0, 6144 + 512 * (g + 1), 512, 8, slot=wv_s)
        if g == 0:
            for which, (wsl, wvw) in (("q", (wq_s, wq_v)), ("k", (wk_s, wk_v))):
                for blk in range(4):
                    proj_group(g, 0, which, blk, wsl, wvw)
        for h in range(4):
            if h < 3:
                gn_, hn_ = g, h + 1
            elif g + 1 < 3:
                gn_, hn_ = g + 1, 0
                wq_s, wq_v = load_w(w_in, 0, 3072 + 512 * (g + 1), 512, 8, slot=wq_s)
                wk_s, wk_v = load_w(w_in, 0, 4608 + 512 * (g + 1), 512, 8, slot=wk_s)
            else:
                gn_, hn_ = None, None
            if gn_ is None:
                filler = [(lambda t=t, a=wk_s, b_=wk_v: tm_tile(2, "k", t, a, b_)) for t in range(16)]
            else:
                filler = [(lambda which=which, blk=blk, wsl=wsl, wvw=wvw, gn_=gn_, hn_=hn_: proj_group(gn_, hn_, which, blk, wsl, wvw))
                          for which, (wsl, wvw) in (("q", (wq_s, wq_v)), ("k", (wk_s, wk_v))) for blk in range(4)]
            if d2d_left:
                issue_d2d(2, d2d_left.pop(0))
            core(g, h, filler)
            if stop == 'a_c%d' % (4 * g + h):
                return nc, P, sp, locals()
    handover(sslot_t, [bank_t[0], bank_t[1]])

    WR_MAP = {0: 4, 1: 5, 2: 2, 3: 3, 4: 0, 5: 1}

    def wr_blk(j):
        o = WR_MAP[j] * 4096
        return R5[:, o:o + 4096].rearrange("p (k n) -> p k n", k=8)
    wr_ts = [Tile("wr%d" % j) for j in range(6)]
    handover([ssb_t[0], ssb_t[1], ssb_t[2], ssb_t[3], ptb_t[0], ptb_t[1], ptb_t[2], ptb_t[3], stg_t[0], stg_t[1],
              ckt_t[0], ckt_t[1], qs_t], [wr_ts[0], wr_ts[1]])
    handover([qT_t[0], qT_t[1], kT_t[0], kT_t[1]], [wr_ts[2], wr_ts[3]])
    for j in range(4):
        pool.dma(wr_blk(j), w_in[0:1024, 512 * j:512 * (j + 1)].rearrange("(k p) n -> p k n", p=128), "wr%d" % j,
                 writes=[wr_ts[j]])
    rtmp = R5f[:, 0:2048]
    rtmp_t = Tile("rtmp")
    handover([vg_t, qT_t[0], qT_t[1], kT_t[0], kT_t[1]], [rtmp_t])
    rtmps = [R5f[:, 0:2048], R5f[:, 2048:4096]]
    rtmp_ts = [rtmp_t, Tile("rtmp1")]
    handover([vg_t, qT_t[0], qT_t[1], kT_t[0], kT_t[1]], [rtmp_ts[1]])
    for h in range(4):
        rt_, rt_t_ = rtmps[h % 2], rtmp_ts[h % 2]
        act.op(lambda e, h=h, rt_=rt_: e.activation(out=rt_, in_=denacc[:, h, 0:2048], func=AF.Ln),
               reads=[den_t[h]], writes=[rt_t_])
        act.op(lambda e, rt_=rt_: e.activation(out=rt_, in_=rt_, func=AF.Exp, scale=-1.0), reads=[rt_t_], writes=[rt_t_])
        dve.op(lambda e, h=h, rt_=rt_: e.tensor_tensor(out=attT[:, h, 0:2048], in0=numacc[:, h, 0:2048], in1=rt_, op=ALU.mult),
               reads=[num_t[h], rt_t_], writes=[attT_t])
    dve.op(lambda e: e.reciprocal(out=sm[:, 96:112], in_=dens), reads=[nds_t], writes=[st_t])
    dve.op(lambda e: e.tensor_tensor(out=attT[:, :, 2048:2052],
                                     in0=nums.rearrange("p (b h) -> p h b", b=4),
                                     in1=sm[:, 96:112].rearrange("p (b h) -> p h b", b=4), op=ALU.mult),
           reads=[nds_t, st_t], writes=[attT_t])

    if dbg:
        dbg_att = dout("dbg_att", [128, 4 * NT], BF16)
        sp.dma(dbg_att[:, :], R3[:, :], "dbg", reads=[attT_t])


    if stop == 'att':
        return nc, P, sp, locals()
    handover([vg_t, rtmp_t, rtmp_ts[1]], [wr_ts[4], wr_ts[5]])
    for j in (4, 5):
        pool.dma(wr_blk(j), w_in[0:1024, 512 * j:512 * (j + 1)].rearrange("(k p) n -> p k n", p=128), "wr%d" % j,
                 writes=[wr_ts[j]])
    R4b = R4[:, :].bitcast(BF16)
    qf, kf = R4[:, 0:512], R4[:, 512:1024]
    tt = [R4[:, 1024 + 256 * i:1024 + 256 * i + 512] for i in range(3)] + [R4[:, 1792:2048]]
    tall = R4[:, 1024:2048]
    gs, on, S32, stt = R4[:, 2048:3072], R4[:, 3072:4096], R4[:, 4096:4608], R4[:, 4608:4672]
    qd, kd, vb = R4b[:, 9344:9856], R4b[:, 9856:10368], R4b[:, 10368:11392]
    qkT = R4b[:, 11392:12416]
    qdT = R4b[:, 11392:11904].rearrange("p (a n) -> p a n", a=4)
    kdT = R4b[:, 11904:12416].rearrange("p (a n) -> p a n", a=4)
    mT = R4b[:, 12416:13440].rearrange("p (h n) -> p h n", h=8)
    ro = R4b[:, 13440:14464]
    Sb = R4b[:, 14464:14976].rearrange("p (a n) -> p a n", a=4)
    rt_names = ["qf", "kf", "tt", "gs", "on", "S32", "stt", "qd", "kd", "vb", "qkT", "mT", "ro", "Sb", "ttk"]
    RT = {n: Tile("r_" + n) for n in rt_names}
    handover(den_t + [tv_t, oh_t], list(RT.values()))
    handover(num_t, [retT_t])
    ttk = [R5f[:, 0:0]]
    WSf = WS[:, :, :].rearrange("p s n -> p (s n)").bitcast(F32)
    tk = [WSf[:, 4096 + 256 * i:4096 + 256 * i + 512] for i in range(3)] + [WSf[:, 4864:5120]]
    sp.dma(gt[:, :], gng[0:1, :].broadcast_to([128, D]), "c_gt", writes=[gt_t])
    dve.op(lambda e: e.memset(S32, 0.0), writes=[RT["S32"]])
    o_cos, o_sin = CF["cos"][0], CF["sin"][0]
    S_s = WSf[0:64, 0:4096]
    Ss4 = S_s.rearrange("p (b h v) -> p b h v", b=4, h=8)
    Vbd = WS[0:4, 2, :].rearrange("p (h b v) -> p h b v", h=8, b=4)
    Ss_t = Tile("S_s")
    Vbd_t = Tile("Vbd")

    def bc_heads(tab, rows):
        return tab.unsqueeze(1).broadcast_to([rows, 8, 32])

    def rotary(eng, src, tmps, c, rows, tile_src, tile_tmp):
        X = src[0:rows, :].rearrange("p (h t f) -> p h t f", h=8, t=2)
        x1, x2 = X[:, :, 0, :], X[:, :, 1, :]
        ct = cf[0:rows, o_cos + 32 * c:o_cos + 32 * (c + 1)]
        st_ = cf[0:rows, o_sin + 32 * c:o_sin + 32 * (c + 1)]
        cs4 = ct.unsqueeze(1).unsqueeze(1).broadcast_to([rows, 8, 2, 32])
        sn = bc_heads(st_, rows)
        t1 = tmps[0][0:rows, :]
        t2 = tmps[2][0:rows, :]
        T1 = t1.rearrange("p (h t f) -> p h t f", h=8, t=2)
        T2 = t2.rearrange("p (h t f) -> p h t f", h=8, t=2)
        eng.op(lambda e: e.tensor_tensor(out=T1, in0=X, in1=cs4, op=ALU.mult), reads=[tile_src, cf_t], writes=[tile_tmp])
        eng.op(lambda e: e.tensor_tensor(out=T2[:, :, 0, :], in0=x2, in1=sn, op=ALU.mult), reads=[tile_src, cf_t], writes=[tile_tmp])
        eng.op(lambda e: e.tensor_tensor(out=T2[:, :, 1, :], in0=x1, in1=sn, op=ALU.mult), reads=[tile_src, cf_t], writes=[tile_tmp])
        eng.op(lambda e: e.tensor_tensor(out=x1, in0=T1[:, :, 0, :], in1=T2[:, :, 0, :], op=ALU.subtract),
               reads=[tile_tmp], writes=[tile_src])
        eng.op(lambda e: e.tensor_tensor(out=x2, in0=T1[:, :, 1, :], in1=T2[:, :, 1, :], op=ALU.add),
               reads=[tile_tmp], writes=[tile_src])

    def dec_bc(name, rows):
        o, w = CF[name]
        return cf[0:rows, o:o + 8].unsqueeze(2).broadcast_to([rows, 8, 64])

    WSb = WS[:, :, :].rearrange("p s n -> p (s n)")
    def gsi(c):
        return 0 if c == 16 else c % 3
    gs3 = [gs, WSf[:, 0:1024], WSf[:, 1024:2048]]
    gs3_t = [RT["gs"], Tile("gs_b"), Tile("gs_c")]
    SA = [
        dict(qd=qd, kd=kd, vb=vb, qkT=qkT, gs=gs),
        dict(qd=WSb[:, 4096:4608], kd=WSb[:, 4608:5120], vb=WSb[:, 5120:6144], qkT=WSb[:, 6144:7168], gs=WSf[:, 0:1024]),
    ]
    SAT = [dict(qd=RT["qd"], kd=RT["kd"], vb=RT["vb"], qkT=RT["qkT"], gs=RT["gs"]),
           {n: Tile("s1_" + n) for n in ("qd", "kd", "vb", "qkT", "gs")}]
    tallB_t = gs3_t[2]
    handover([ws_t[0], ws_t[1]], list(SAT[1].values()) + [gs3_t[1], gs3_t[2]])
    o_g = CF["gamC"][0]
    o_bd = CF["bd"][0]
    lgam = [math.exp(math.log1p(-2.0 ** (-5 - h))) for h in range(8)]

    def stage_A(c, part):
        rows = 128 if c < 16 else 4
        tok = slice(c * 128, c * 128 + rows)
        s_ = c % 2
        B_, T_ = SA[s_], SAT[s_]
        abank = [0, 1, 2, 0, 1, 2]
        for j in (range(4) if part == 0 else range(4, 6)):
            bj = abank[j]
            for k in range(8):
                pe.op(lambda e, j=j, k=k: e.matmul(ps[0:rows, bj, :], lhsT=hT[:, k, tok], rhs=wr_blk(j)[:, k, :],
                                                   start=(k == 0), stop=(k == 7)),
                      reads=[hT_t, wr_ts[j]], writes=[bank_t[bj]], signal=(k == 7))
            if j == 0:
                act.op(lambda e: e.copy(out=qf[0:rows, :], in_=ps[0:rows, bj, :]), reads=[bank_t[bj]], writes=[RT["qf"]])
            elif j == 1:
                act.op(lambda e: e.copy(out=kf[0:rows, :], in_=ps[0:rows, bj, :]), reads=[bank_t[bj]], writes=[RT["kf"]])
            elif j in (2, 3):
                act.op(lambda e, j=j: e.copy(out=B_["vb"][0:rows, 512 * (j - 2):512 * (j - 1)], in_=ps[0:rows, bj, :]),
                       reads=[bank_t[bj]], writes=[T_["vb"]])
            else:
                act.op(lambda e, j=j: e.activation(out=gs3[gsi(c)][0:rows, 512 * (j - 4):512 * (j - 3)], in_=ps[0:rows, bj, :],
                                                   func=AF.Silu), reads=[bank_t[bj]], writes=[gs3_t[gsi(c)]])
        if part == 0:
            rotary(dve, qf, tt, c, rows, RT["qf"], RT["tt"])
            rotary(pool, kf, tk, c, rows, RT["kf"], ws_t[2])
        if c < 16 and part == 0:
            dve.op(lambda e: e.tensor_tensor(out=B_["qd"].rearrange("p (h f) -> p h f", h=8),
                                             in0=qf.rearrange("p (h f) -> p h f", h=8), in1=dec_bc("qdec", 128), op=ALU.mult),
                   reads=[RT["qf"], cf_t], writes=[T_["qd"]])
            pool.op(lambda e: e.tensor_tensor(out=B_["kd"].rearrange("p (h f) -> p h f", h=8),
                                              in0=kf.rearrange("p (h f) -> p h f", h=8), in1=dec_bc("kdec", 128), op=ALU.mult),
                    reads=[RT["kf"], cf_t], writes=[T_["kd"]])
        if part == 0:
            return
        if c < 16:
            for i, srcb in enumerate((B_["qd"], B_["kd"])):
                for pr in range(4):
                    pe.op(lambda e, i=i, pr=pr, srcb=srcb: e.transpose(
                        out=bankbf(0)[:, 512 * i + 128 * pr:512 * i + 128 * (pr + 1)], in_=srcb[:, 128 * pr:128 * (pr + 1)],
                        identity=identb), reads=[T_["qd"], T_["kd"], cb_t], writes=[bank_t[0]], signal=(i == 1 and pr == 3))
            act.op(lambda e: e.copy(out=B_["qkT"], in_=bankbf(0)), reads=[bank_t[0]], writes=[T_["qkT"]])
        pool.op(lambda e: e.tensor_tensor(out=gs3[gsi(c)][0:rows, :], in0=gs3[gsi(c)][0:rows, :], in1=gt[0:rows, :], op=ALU.mult),
                reads=[gs3_t[gsi(c)], gt_t], writes=[gs3_t[gsi(c)]])

    def stage_B(c):
        rows = 128 if c < 16 else 4
        tok = slice(c * 128, c * 128 + rows)
        s_ = c % 2
        B_, T_ = SA[s_], SAT[s_]
        vb_, kd_ = B_["vb"], B_["kd"]
        qdT_ = B_["qkT"][:, 0:512].rearrange("p (a n) -> p a n", a=4)
        kdT_ = B_["qkT"][:, 512:1024].rearrange("p (a n) -> p a n", a=4)
        if c < 16:
            for h in range(8):
                base, pr = (h % 2) * 64, h // 2
                pe.op(lambda e, h=h, base=base, pr=pr: e.matmul(
                    ps[:, 3 + h % 2, 128 * pr:128 * (pr + 1)], lhsT=kdT_[base:base + 64, pr, :], rhs=qdT_[base:base + 64, pr, :],
                    start=True, stop=True), reads=[T_["qkT"]], writes=[bank_t[3 + h % 2]], signal=(h >= 6))
            for par in range(2):
                dve.op(lambda e, par=par: e.tensor_tensor(
                    out=mT[:, par::2, :], in0=bank(3 + par).rearrange("p (a n) -> p a n", a=4),
                    in1=causal.unsqueeze(1).broadcast_to([128, 4, 128]), op=ALU.mult),
                    reads=[bank_t[3 + par], cb_t], writes=[RT["mT"]])
            for h in range(8):
                base, pr = (h % 2) * 64, h // 2
                ob_ = 5 + h // 4
                oc = (h % 4) * 128
                pe.op(lambda e, h=h, ob_=ob_, oc=oc: e.matmul(
                    ps[:, ob_, oc:oc + 128], lhsT=mT[:, h, :], rhs=vb_[:, 128 * h:128 * (h + 1)], start=True, stop=(c == 0)),
                    reads=[RT["mT"], T_["vb"]], writes=[bank_t[ob_]], signal=(c == 0 and h % 4 == 3))
                if c > 0:
                    pe.op(lambda e, h=h, ob_=ob_, oc=oc, base=base, pr=pr: e.matmul(
                        ps[:, ob_, oc:oc + 128], lhsT=qdT_[:, pr, :], rhs=Sbz[:, h, :],
                        start=False, stop=True), reads=[T_["qkT"], RT["Sb"]], writes=[bank_t[ob_]], signal=(h % 4 == 3))
            for h in range(8):
                base, pr = (h % 2) * 64, h // 2
                pe.op(lambda e, h=h, base=base, pr=pr: e.matmul(
                    ps[base:base + 64, 7, 128 * pr:128 * (pr + 1)], lhsT=kd_[:, 64 * h:64 * (h + 1)],
                    rhs=vb_[:, 128 * h:128 * (h + 1)], start=True, stop=True),
                    reads=[T_["kd"], T_["vb"]], writes=[bank_t[7]], signal=(h == 7))
            return

    def stage_Bb(c):
        if True:
            act.op(lambda e: e.copy(out=on, in_=ps[:, 5:7, :].rearrange("p a n -> p (a n)")),
                   reads=[bank_t[5], bank_t[6]], writes=[RT["on"]])
            dve.op(lambda e: e.tensor_tensor(out=S32, in0=bank(7), in1=S32, op=ALU.add),
                   reads=[bank_t[7], RT["S32"]], writes=[RT["S32"]])
            dve.op(lambda e: e.tensor_tensor(out=S32.rearrange("p (a n) -> p a n", a=4),
                                             in0=S32.rearrange("p (a n) -> p a n", a=4),
                                             in1=cf[:, o_g:o_g + 4].unsqueeze(2).broadcast_to([128, 4, 128]), op=ALU.mult),
                   reads=[RT["S32"], cf_t], writes=[RT["S32"]])
            for e2 in range(2):
                act.op(lambda e, e2=e2: e.copy(out=Sbz[64 * e2:64 * (e2 + 1), e2::2, :],
                                               in_=S32[64 * e2:64 * (e2 + 1), :].rearrange("p (a n) -> p a n", a=4)),
                       reads=[RT["S32"]], writes=[RT["Sb"]])

    def stage_Bs(c):
        rows = 4
        tok = slice(c * 128, c * 128 + rows)
        s_ = c % 2
        B_, T_ = SA[s_], SAT[s_]
        vb_, kd_ = B_["vb"], B_["kd"]
        if True:
            pool.op(lambda e: e.tensor_tensor(out=kd_[0:4, :].rearrange("p (h f) -> p h f", h=8),
                                              in0=kf[0:4, :].rearrange("p (h f) -> p h f", h=8), in1=dec_bc("kdec_s", 4),
                                              op=ALU.mult), reads=[RT["kf"], cf_t], writes=[T_["kd"]])
            for b in range(4):
                dve.op(lambda e, b=b: e.tensor_scalar(out=Vbd[:, :, b, :], in0=vb_[0:4, :].rearrange("p (h v) -> p h v", h=8),
                                                      scalar1=cf[0:4, o_bd + b:o_bd + b + 1], scalar2=None, op0=ALU.mult),
                       reads=[T_["vb"], cf_t], writes=[Vbd_t, ws_t[2], RT["Sb"]])
            for h in range(8):
                bk_ = 3 + h % 2
                pe.op(lambda e, h=h, bk_=bk_: e.matmul(ps[0:64, bk_, :], lhsT=kd_[0:4, 64 * h:64 * (h + 1)],
                                                       rhs=Vbd[:, h, :, :], start=True, stop=True),
                      reads=[T_["kd"], Vbd_t], writes=[bank_t[bk_]])
                dve.op(lambda e, h=h, bk_=bk_: e.scalar_tensor_tensor(
                    out=Ss4[:, :, h, :], in0=Ss4[:, :, h, :], scalar=lgam[h],
                    in1=ps[0:64, bk_, :].rearrange("p (b v) -> p b v", b=4), op0=ALU.mult, op1=ALU.add),
                    reads=[bank_t[bk_], Ss_t], writes=[Ss_t])
            sp.dma(sts.rearrange("b h d v -> d (b h) v"), S_s.rearrange("p (m v) -> p m v", v=128), "c_so", reads=[Ss_t])
            for h in range(8):
                pe.op(lambda e, h=h: e.transpose(out=ps[0:64, 7, 4 * h:4 * h + 4], in_=qf[0:4, 64 * h:64 * (h + 1)],
                                                 identity=identf[0:4, 0:4]),
                      reads=[RT["qf"], cf_t], writes=[bank_t[7]], signal=(h == 7))
            qsT = stt[0:64, 0:32]
            act.op(lambda e: e.copy(out=qsT, in_=ps[0:64, 7, 0:32]), reads=[bank_t[7]], writes=[RT["stt"]])
            for h in range(8):
                bk_ = 5 + h % 2
                pe.op(lambda e, h=h, bk_=bk_: e.matmul(ps[0:4, bk_, :], lhsT=qsT[:, 4 * h:4 * h + 4], rhs=Ss4[:, :, h, :],
                                                       start=True, stop=True),
                      reads=[RT["stt"], Ss_t], writes=[bank_t[bk_]])
                dve.op(lambda e, bk_=bk_: e.tensor_tensor(
                    out=tall[0:4, 0:512].rearrange("p (b v) -> p b v", b=4),
                    in0=ps[0:4, bk_, :].rearrange("p (b v) -> p b v", b=4),
                    in1=cf[0:4, o_bd:o_bd + 4].unsqueeze(2).broadcast_to([4, 4, 128]), op=ALU.mult),
                    reads=[bank_t[bk_], cf_t], writes=[RT["tt"]])
                dve.op(lambda e, h=h: e.tensor_reduce(out=on[0:4, 128 * h:128 * (h + 1)],
                                                      in_=tall[0:4, 0:512].rearrange("p (b v) -> p v b", b=4),
                                                      axis=AX.X, op=ALU.add), reads=[RT["tt"]], writes=[RT["on"]])

    def stage_B2(c):
        rows = 128 if c < 16 else 4
        s_ = c % 2
        B_, T_ = SA[s_], SAT[s_]
        ro, ro_t = ros[c % 2], ro_ts[c % 2]
        sq_tmp, sq_t = tall, RT["tt"]
        on3 = on[0:rows, :].rearrange("p (h v) -> p h v", h=8)
        sA, sB, sC, sD = tuple(stt[0:rows, 32 + 8 * i:40 + 8 * i] for i in range(4))
        dve.op(lambda e: e.tensor_reduce(out=sA, in_=on3, axis=AX.X, op=ALU.add), reads=[RT["on"]], writes=[RT["stt"]])
        act.op(lambda e: e.activation(out=sq_tmp[0:rows, :], in_=on[0:rows, :], func=AF.Square),
               reads=[RT["on"]], writes=[sq_t])
        dve.op(lambda e: e.tensor_reduce(out=sB, in_=sq_tmp[0:rows, :].rearrange("p (h v) -> p h v", h=8), axis=AX.X,
                                         op=ALU.add), reads=[sq_t], writes=[RT["stt"]])
        dve.op(lambda e: e.tensor_scalar(out=sA, in0=sA, scalar1=1.0 / 128, scalar2=None, op0=ALU.mult),
               reads=[RT["stt"]], writes=[RT["stt"]])
        dve.op(lambda e: e.tensor_tensor(out=sC, in0=sA, in1=sA, op=ALU.mult), reads=[RT["stt"]], writes=[RT["stt"]])
        dve.op(lambda e: e.scalar_tensor_tensor(out=sB, in0=sB, scalar=1.0 / 128, in1=sC, op0=ALU.mult, op1=ALU.subtract),
               reads=[RT["stt"]], writes=[RT["stt"]])
        act.op(lambda e: e.activation(out=sC, in_=sB, func=AF.Sqrt, bias=epsc[0:rows, :], scale=1.0),
               reads=[RT["stt"], cf_t], writes=[RT["stt"]])
        dve.op(lambda e: e.reciprocal(out=sD, in_=sC), reads=[RT["stt"]], writes=[RT["stt"]])
        dve.op(lambda e: e.tensor_tensor(out=on3, in0=on3, in1=sA.unsqueeze(2).broadcast_to([rows, 8, 128]), op=ALU.subtract),
               reads=[RT["on"], RT["stt"]], writes=[RT["on"]])
        dve.op(lambda e: e.tensor_tensor(out=on3, in0=on3, in1=sD.unsqueeze(2).broadcast_to([rows, 8, 128]), op=ALU.mult),
               reads=[RT["on"], RT["stt"]], writes=[RT["on"]])
        pool.op(lambda e: e.tensor_tensor(out=ro[0:rows, :], in0=on[0:rows, :], in1=gs3[gsi(c)][0:rows, :], op=ALU.mult),
                reads=[RT["on"], gs3_t[gsi(c)]], writes=[ro_t])

    def stage_C(c):
        rows = 128 if c < 16 else 4
        tok = slice(c * 128, c * 128 + rows)
        ro, ro_t = ros[c % 2], ro_ts[c % 2]
        for h in range(8):
            pe.op(lambda e, h=h: e.transpose(out=bankbf(3)[:, 128 * h:128 * h + rows], in_=ro[0:rows, 128 * h:128 * (h + 1)],
                                             identity=identb[0:rows, 0:rows]),
                  reads=[ro_t, cb_t], writes=[bank_t[3]], signal=(h == 7))
        act.op(lambda e: e.copy(out=retT[:, :, tok], in_=bankbf(3).rearrange("p (k c) -> p k c", k=8)[:, :, 0:rows]),
               reads=[bank_t[3]], writes=[retT_t])
        if c == 15:
            for hh in range(2):
                sp.dma(stp.rearrange("(pr two) d v -> two d pr v", two=2)[hh],
                       S32[64 * hh:64 * (hh + 1), :].rearrange("p (a n) -> p a n", a=4), "c_sp%d" % hh, reads=[RT["S32"]])

    Sbz = WSb[:, 10240:11264].rearrange("p (h n) -> p h n", h=8)
    dve.op(lambda e: e.memset(WSb[:, 10240:11264], 0.0), writes=[RT["Sb"], ws_t[2]])
    ros = [ro, R4b[:, 14976:16000]]
    ro_ts = [RT["ro"], Tile("ro2")]
    handover(den_t + [tv_t, oh_t], [ro_ts[1]])
    wab = R5[:, 19488:23584].rearrange("p (k n) -> p k n", k=4)
    wab_t = Tile("wab")
    t1s0_t = Tile("t1s0")
    t1s0 = [R5[:, 16416 + wi * 1024:16416 + (wi + 1) * 1024].rearrange("p (k n) -> p k n", k=8) for wi in range(3)]
    stage_A(0, 0)
    stage_A(0, 1)
    for c in range(16):
        if c + 1 < 16:
            stage_A(c + 1, 0)
            stage_A(c + 1, 1)
        stage_B(c)
        if c >= 1:
            stage_B2(c - 1)
        stage_Bb(c)
        if c >= 2:
            stage_C(c - 2)
    stage_B2(15)
    stage_C(14)
    sp.dma(S_s.rearrange("p (m v) -> p m v", v=128), st_in.rearrange("b h d v -> d (b h) v"), "c_ss",
           writes=[Ss_t, ws_t[0], ws_t[1], gs3_t[1], gs3_t[2]] + list(SAT[1].values()))
    stage_A(16, 0)
    stage_A(16, 1)
    handover(wr_ts, [wab_t, t1s0_t])
    pool.dma(wab, w_ab[:, :].rearrange("(k p) n -> p k n", p=128), "wab", writes=[wab_t])
    for wi, (src_, c0_) in enumerate(((w_rb, 0), (w_in, 7680), (w_in, 8704))):
        pool.dma(t1s0[wi], src_[0:1024, c0_:c0_ + 128].rearrange("(k p) n -> p k n", p=128), "t1s0", writes=[t1s0_t])
    stage_Bs(16)
    stage_B2(16)
    stage_C(15)
    stage_C(16)
    if dbg:
        dbg_ret = dout("dbg_ret", [128, 8 * NT], BF16)
        sp.dma(dbg_ret[:, :], R2b, "dbg", reads=[retT_t])
    if stop == 'ret':
        return nc, P, sp, locals()


    mixedT = R5[:, 0:8 * NT].rearrange("p (k n) -> p k n", k=8)
    mixed_t = Tile("mixedT")
    handover(wr_ts, [mixed_t])
    T1t_all = [R4[:, 512 * i:512 * (i + 1)] for i in range(8)]
    T1_t_all = [Tile("T1t%d" % i) for i in range(8)]
    T1_t = T1_t_all
    handover(list(RT.values()), T1_t)
    ws6_t = [Tile("ws6_%d" % i) for i in range(4)]
    handover([ws_t[0], ws_t[1], ws_t[2], Ss_t, Vbd_t, gs3_t[1], gs3_t[2]] + list(SAT[1].values()), ws6_t)

    def load_t1(f):
        s6 = (f - 1) % 4
        views = []
        for wi, (src, c0) in enumerate(((w_rb, 128 * f), (w_in, 7680 + 128 * f), (w_in, 8704 + 128 * f))):
            v = WSb[:, s6 * 3072 + wi * 1024:s6 * 3072 + (wi + 1) * 1024].rearrange("p (k n) -> p k n", k=8)
            pool.dma(v, src[0:1024, c0:c0 + 128].rearrange("(k p) n -> p k n", p=128), "ws6_%d" % s6, writes=[ws6_t[s6]])
            views.append(v)
        return views
    t1w = {0: t1s0}
    for f_ in range(1, 4):
        t1w[f_] = load_t1(f_)
    it = 0
    for f in range(8):
        if f >= 1 and f + 3 < 8:
            t1w[f + 3] = load_t1(f + 3)
        v_rb, v_gr, v_ga = t1w[f]
        w6 = t1s0_t if f == 0 else ws6_t[(f - 1) % 4]
        if True:
            fs = slice(0, 128)
            for (t0, w) in TB:
                bo_ = 4 * (it % 2)
                T1t = T1t_all[4 * (it % 2):4 * (it % 2) + 4]
                T1_t = T1_t_all[4 * (it % 2):4 * (it % 2) + 4]
                it += 1
                ts = slice(t0, t0 + w)
                for k in range(8):
                    pe.op(lambda e, k=k: e.matmul(ps[:, bo_, 0:w], lhsT=v_rb[:, k, fs], rhs=retT[:, k, ts], start=(k == 0), stop=(k == 7)),
                          reads=[w6, retT_t], writes=[bank_t[bo_]], signal=(k == 7))
                for k in range(4):
                    pe.op(lambda e, k=k: e.matmul(ps[:, bo_ + 1, 0:w], lhsT=wab[:, k, 128 * f:128 * (f + 1)], rhs=attT[:, k, ts], start=(k == 0), stop=(k == 3)),
                          reads=[wab_t, attT_t], writes=[bank_t[bo_ + 1]], signal=(k == 3))
                for k in range(8):
                    pe.op(lambda e, k=k: e.matmul(ps[:, bo_ + 2, 0:w], lhsT=v_gr[:, k, fs], rhs=hT[:, k, ts], start=(k == 0), stop=(k == 7)),
                          reads=[w6, hT_t], writes=[bank_t[bo_ + 2]], signal=(k == 7))
                for k in range(8):
                    pe.op(lambda e, k=k: e.matmul(ps[:, bo_ + 3, 0:w], lhsT=v_ga[:, k, fs], rhs=hT[:, k, ts], start=(k == 0), stop=(k == 7)),
                          reads=[w6, hT_t], writes=[bank_t[bo_ + 3]], signal=(k == 7))
                act.op(lambda e: e.activation(out=T1t[0][:, 0:w], in_=ps[:, bo_ + 2, 0:w], func=AF.Sigmoid),
                       reads=[bank_t[bo_ + 2]], writes=[T1_t[0]])
                act.op(lambda e: e.activation(out=T1t[1][:, 0:w], in_=ps[:, bo_ + 3, 0:w], func=AF.Sigmoid),
                       reads=[bank_t[bo_ + 3]], writes=[T1_t[1]])
                dve.op(lambda e: e.tensor_tensor(out=T1t[2][:, 0:w], in0=ps[:, bo_, 0:w], in1=T1t[0][:, 0:w], op=ALU.mult),
                       reads=[bank_t[bo_], T1_t[0]], writes=[T1_t[2]])
                dve.op(lambda e: e.tensor_tensor(out=T1t[3][:, 0:w], in0=ps[:, bo_ + 1, 0:w], in1=T1t[1][:, 0:w], op=ALU.mult),
                       reads=[bank_t[bo_ + 1], T1_t[1]], writes=[T1_t[3]])
                dve.op(lambda e: e.tensor_tensor(out=mixedT[:, f, ts], in0=T1t[2][:, 0:w], in1=T1t[3][:, 0:w], op=ALU.add),
                       reads=[T1_t[2], T1_t[3]], writes=[mixed_t])

    xlo = R2[:, :].rearrange("p (k n) -> p k n", k=4)
    xhi = R4[:, :].rearrange("p (k n) -> p k n", k=4)

    def xacc(f):
        return (xlo if f < 4 else xhi)[:, f % 4, :]
    xa_t = [Tile("xa%d" % f) for f in range(8)]
    handover([retT_t], xa_t[0:4])
    handover(T1_t_all, xa_t[4:8])
    scr = [R5f[:, 8208:9232], R5f[:, 9232:10256]]
    scr_t = [Tile("scr0"), Tile("scr1")]
    sqbs = [R5[:, 22560:23072], R5[:, 23584:24096]]
    sqb_ts = [Tile("sqb0"), Tile("sqb1")]
    handover([wab_t, t1s0_t], scr_t + sqb_ts)
    junk2 = R5[:, 22560:23584]
    scr4 = scr + [R5f[:, 10256:11280]]
    scr4_t = scr_t + [Tile("scr2")]
    handover([wab_t, t1s0_t], [scr4_t[2]])
    for t in range(17):
        rows = 128 if t < 16 else 4
        sl = t % 3
        tok = slice(t * 128, t * 128 + rows)
        src = x[t * 128:(t + 1) * 128, :] if t < 16 else xs[:, :]
        sp.dma(scr4[sl][0:rows, :], src, "scr%d" % sl, writes=[scr4_t[sl]])
        bp = 2 * (t % 4)
        for k in range(8):
            pe.op(lambda e, k=k: e.transpose(out=ps[:, bp + k // 4, 128 * (k % 4):128 * (k % 4) + rows],
                                             in_=scr4[sl][0:rows, 128 * k:128 * (k + 1)], identity=identf[0:rows, 0:rows]),
                  reads=[scr4_t[sl], cf_t], writes=[bank_t[bp], bank_t[bp + 1]], signal=(k == 7))
        for half, xv in ((0, xlo), (1, xhi)):
            act.op(lambda e, half=half, xv=xv: e.copy(out=xv[:, :, tok],
                                                      in_=ps[:, bp + half, :].rearrange("p (k c) -> p k c", k=4)[:, :, 0:rows]),
                   reads=[bank_t[bp + half]], writes=xa_t[4 * half:4 * half + 4])
    handover(ws6_t, [ws_t[0], ws_t[1], ws_t[2]])
    s_o0, v_o0 = load_w(w_o, 0, 0, 512, 8, slot=0)
    s_o1, v_o1 = load_w(w_o, 0, 512, 512, 8, slot=1)
    mlp_pre = load_w(w_up, 0, 0, 512, 8, slot=2)
    v_o = (v_o0, v_o1)
    h2T = hT
    h2_t = Tile("h2T")
    handover([hT_t], [h2_t])
    R3f = R3[:, :].bitcast(F32)
    rstd = R3f[:, 0:NT]
    rstd_t = Tile("rstd")
    handover([attT_t], [rstd_t])
    g2c = sm[:, 120:128]
    g2_t = Tile("g2c")
    sp.dma(sm[0:8, 128:256], ln2g.rearrange("(f p) -> f p", p=128), "c_g2", writes=[g2_t])
    pe.op(lambda e: e.transpose(out=ps[:, 7, 0:8], in_=sm[0:8, 128:256], identity=identf[0:8, 0:8]),
          reads=[g2_t, cf_t], writes=[bank_t[7]])
    act.op(lambda e: e.copy(out=g2c, in_=ps[:, 7, 0:8]), reads=[bank_t[7]], writes=[g2_t])
    def h2_scale(f, tsl):
        dve.op(lambda e: e.scalar_tensor_tensor(out=h2T[:, f, tsl], in0=xacc(f)[:, tsl], scalar=g2c[:, f:f + 1],
                                                in1=rstd[:, tsl], op0=ALU.mult, op1=ALU.mult),
               reads=[xa_t[f], g2_t, rstd_t], writes=[h2_t])
    it = 0
    prev_tb = None
    for (t0, w) in TB:
        ts = slice(t0, t0 + w)
        for f in range(8):
            bo_ = it % 4
            it += 1
            if prev_tb is not None:
                h2_scale(f, prev_tb)
            for k in range(8):
                pe.op(lambda e, k=k: e.matmul(ps[:, bo_, 0:w], lhsT=v_o[f // 4][:, k, 128 * (f % 4):128 * (f % 4 + 1)],
                                              rhs=mixedT[:, k, ts], start=(k == 0), stop=(k == 7)),
                      reads=[ws_t[f // 4], mixed_t], writes=[bank_t[bo_]], signal=(k == 7))
            dve.op(lambda e: e.tensor_tensor(out=xacc(f)[:, ts], in0=ps[:, bo_, 0:w], in1=xacc(f)[:, ts], op=ALU.add),
                   reads=[bank_t[bo_], xa_t[f]], writes=[xa_t[f]])
            sqb, sqb_t = sqbs[f % 2], sqb_ts[f % 2]
            act.op(lambda e: e.activation(out=sqb[:, 0:w], in_=xacc(f)[:, ts], func=AF.Square),
                   reads=[xa_t[f]], writes=[sqb_t])
            if f > 0:
                pe.op(lambda e, f=f: e.matmul(ps[:, 6, 0:w], lhsT=onesb, rhs=sqbs[(f - 1) % 2][:, 0:w], start=(f == 1), stop=False),
                      reads=[sqb_ts[(f - 1) % 2], cb_t], writes=[bank_t[6]], signal=True)
        pe.op(lambda e: e.matmul(ps[:, 6, 0:w], lhsT=onesb, rhs=sqbs[1][:, 0:w], start=False, stop=True),
              reads=[sqb_ts[1], cb_t], writes=[bank_t[6]], signal=True)
        act.op(lambda e: e.activation(out=rstd[:, ts], in_=ps[:, 6, 0:w], func=AF.Ln, bias=epsc, scale=1.0 / D),
               reads=[bank_t[6], cf_t], writes=[rstd_t])
        act.op(lambda e: e.activation(out=rstd[:, ts], in_=rstd[:, ts], func=AF.Exp, scale=-0.5), reads=[rstd_t], writes=[rstd_t])
        prev_tb = ts
    for f in range(8):
        h2_scale(f, prev_tb)

    pleT = R3[:, 2 * NT:4 * NT].rearrange("p (k n) -> p k n", k=2)
    ple_t = Tile("pleT")
    handover([attT_t], [ple_t])
    ppb = R5[:, 20512:22560]
    ppb_t = Tile("ppb")
    handover([scr4_t[2]], [ppb_t])
    for hf in range(3):
        if hf < 2:
            pool.dma(ppb.rearrange("p (t c) -> p t c", t=8),
                     pp[1024 * hf:1024 * (hf + 1), :].rearrange("(t p) c -> p t c", p=128), "ppb", writes=[ppb_t])
            tl = [(8 * hf + i, i, 128) for i in range(8)]
        else:
            pool.dma(ppb[0:4, 0:256], psm[:, :], "ppb", writes=[ppb_t])
            tl = [(16, 0, 4)]
        for (t, i, rows) in tl:
            for cc in range(2):
                pe.op(lambda e, i=i, cc=cc, rows=rows: e.transpose(
                    out=bankbf(6 + i // 4)[:, 256 * (i % 4) + 128 * cc:256 * (i % 4) + 128 * cc + rows],
                    in_=ppb[0:rows, 256 * i + 128 * cc:256 * i + 128 * (cc + 1)], identity=identb[0:rows, 0:rows]),
                    reads=[ppb_t, cb_t], writes=[bank_t[6 + i // 4]], signal=(cc == 1))
            act.op(lambda e, t=t, i=i, rows=rows: e.copy(
                out=pleT[:, :, t * 128:t * 128 + rows],
                in_=bankbf(6 + i // 4)[:, 256 * (i % 4):256 * (i % 4 + 1)].rearrange("p (c n) -> p c n", c=2)[:, :, 0:rows]),
                reads=[bank_t[6 + i // 4]], writes=[ple_t])
    wpl_t = Tile("wpl")
    handover([rstd_t], [wpl_t])
    v_pl = R3[:, 0:2048].rearrange("p (k n) -> p k n", k=2)
    pool.dma(v_pl, w_pl[:, :].rearrange("(k p) n -> p k n", p=128), "wpl", writes=[wpl_t])
    rTb = [R5[:, 8208 * i:8208 * (i + 1)].rearrange("p (k n) -> p k n", k=4) for i in range(2)]
    rT_t = [Tile("rT0"), Tile("rT1")]
    handover([mixed_t], rT_t)
    it = 0
    for j in range(8):
        if j == 0:
            su, vu = mlp_pre
            ws_state["n"] = 0
        else:
            su, vu = load_w(w_up, 0, 512 * j, 512, 8)
        sd_, vd_ = load_w(w_dn, 512 * j, 0, 1024, 4)
        if j == 7:
            free_slot = 3 - su - sd_
            s_g0, v_g0 = load_w(w_pg, 0, 0, 512, 8, slot=free_slot)
        issue_d2d(j // 4, j % 4)
        W_ = WINS[2]
        dst_, src_ = ((ks[2], ck[2]), (vs[2], cv[2]))[j % 2]
        act.dma(dst_[j // 2, 0:W_ - 1, :].rearrange("w c -> (w c)").rearrange("(a n) -> a n", a=16),
                src_[j // 2, 1:W_, :].rearrange("w c -> (w c)").rearrange("(a n) -> a n", a=16), "d2d")
        rb = j % 2
        for fi in range(4):
            for (t0, w) in TB:
                ts = slice(t0, t0 + w)
                bo_ = it % 4
                it += 1
                sl = it % 2
                for k in range(8):
                    pe.op(lambda e, k=k: e.matmul(ps[:, bo_, 0:w], lhsT=vu[:, k, 128 * fi:128 * (fi + 1)], rhs=h2T[:, k, ts],
                                                  start=(k == 0), stop=(k == 7)),
                          reads=[ws_t[su], h2_t], writes=[bank_t[bo_]], signal=(k == 7))
                act.op(lambda e: e.activation(out=scr[sl][:, 0:w], in_=ps[:, bo_, 0:w], func=AF.Relu),
                       reads=[bank_t[bo_]], writes=[scr_t[sl]])
                dve.op(lambda e: e.tensor_tensor(out=rTb[rb][:, fi, ts], in0=ps[:, bo_, 0:w], in1=scr[sl][:, 0:w], op=ALU.mult),
                       reads=[bank_t[bo_], scr_t[sl]], writes=[rT_t[rb]])
        if j == 7:
            s_g1, v_g1 = load_w(w_pg, 0, 512, 512, 8, slot=su)
        for f in range(8):
            for (t0, w) in TB:
                ts = slice(t0, t0 + w)
                bo_ = 4 + it % 4
                it += 1
                for k in range(4):
                    pe.op(lambda e, k=k: e.matmul(ps[:, bo_, 0:w], lhsT=vd_[:, k, 128 * f:128 * (f + 1)], rhs=rTb[rb][:, k, ts],
                                                  start=(k == 0), stop=(k == 3)),
                          reads=[ws_t[sd_], rT_t[rb]], writes=[bank_t[bo_]], signal=(k == 3))
                dve.op(lambda e: e.tensor_tensor(out=xacc(f)[:, ts], in0=ps[:, bo_, 0:w], in1=xacc(f)[:, ts], op=ALU.add),
                       reads=[bank_t[bo_], xa_t[f]], writes=[xa_t[f]])

    x2b = hT
    x2b_t = Tile("x2b")
    handover([h2_t], [x2b_t])
    for f in range(8):
        act.op(lambda e, f=f: e.copy(out=x2b[:, f, :], in_=xacc(f)), reads=[xa_t[f]], writes=[x2b_t])
    v_g = (v_g0, v_g1)
    s_g = (s_g0, s_g1)
    it = 0
    for (t0, w) in TB:
        ts = slice(t0, t0 + w)
        for f in range(8):
            bo_ = 2 * (it % 3)
            it += 1
            sl = it % 2
            for k in range(8):
                pe.op(lambda e, k=k: e.matmul(ps[:, bo_, 0:w], lhsT=v_g[f // 4][:, k, 128 * (f % 4):128 * (f % 4 + 1)],
                                              rhs=x2b[:, k, ts], start=(k == 0), stop=(k == 7)),
                      reads=[ws_t[s_g[f // 4]], x2b_t], writes=[bank_t[bo_]], signal=(k == 7))
            for k in range(2):
                pe.op(lambda e, k=k: e.matmul(ps[:, bo_ + 1, 0:w], lhsT=v_pl[:, k, 128 * f:128 * (f + 1)], rhs=pleT[:, k, ts],
                                              start=(k == 0), stop=(k == 1)),
                      reads=[wpl_t, ple_t], writes=[bank_t[bo_ + 1]], signal=(k == 1))
            act.op(lambda e: e.activation(out=scr[sl][:, 0:w], in_=ps[:, bo_, 0:w], func=AF.Sigmoid),
                   reads=[bank_t[bo_]], writes=[scr_t[sl]])
            dve.op(lambda e: e.tensor_tensor(out=scr[sl][:, 0:w], in0=ps[:, bo_ + 1, 0:w], in1=scr[sl][:, 0:w], op=ALU.mult),
                   reads=[bank_t[bo_ + 1], scr_t[sl]], writes=[scr_t[sl]])
            dve.op(lambda e: e.tensor_tensor(out=xacc(f)[:, ts], in0=xacc(f)[:, ts], in1=scr[sl][:, 0:w], op=ALU.add),
                   reads=[xa_t[f], scr_t[sl]], writes=[xa_t[f]])

    sp.dma(gt[:, :], lnfg[0:1, :].broadcast_to([128, D]), "c_gt", writes=[gt_t])
    junk2_t = Tile("junk2")
    st5_t = [Tile("st5a"), Tile("st5b")]
    for t in range(17):
        rows = 128 if t < 16 else 4
        tok = slice(t * 128, t * 128 + rows)
        bp = 2 * (t % 4)
        sl = t % 2
        for f in range(8):
            pe.op(lambda e, f=f: e.transpose(out=ps[0:rows, bp + f // 4, 128 * (f % 4):128 * (f % 4 + 1)], in_=xacc(f)[:, tok],
                                             identity=identf),
                  reads=[xa_t[f], cf_t], writes=[bank_t[bp], bank_t[bp + 1]], signal=(f == 7))
        yv = ps[0:rows, bp:bp + 2, :].rearrange("p a n -> p (a n)")
        c5 = 100 + 4 * sl
        st5 = st5_t[sl]
        act.op(lambda e: e.activation(out=junk2[0:rows, :], in_=yv, func=AF.Square, accum_out=sm[0:rows, c5:c5 + 1]),
               reads=[bank_t[bp], bank_t[bp + 1]], writes=[junk2_t, st5])
        act.op(lambda e: e.activation(out=sm[0:rows, c5 + 1:c5 + 2], in_=sm[0:rows, c5:c5 + 1], func=AF.Sqrt, bias=epsc[0:rows, :],
                                      scale=1.0 / D), reads=[st5, cf_t], writes=[st5])
        dve.op(lambda e: e.reciprocal(out=sm[0:rows, c5 + 2:c5 + 3], in_=sm[0:rows, c5 + 1:c5 + 2]), reads=[st5], writes=[st5])
        dve.op(lambda e: e.scalar_tensor_tensor(out=scr[sl][0:rows, :], in0=yv, scalar=sm[0:rows, c5 + 2:c5 + 3], in1=gt[0:rows, :],
                                                op0=ALU.mult, op1=ALU.mult),
               reads=[bank_t[bp], bank_t[bp + 1], st5, gt_t], writes=[scr_t[sl]])
        dst = y[t * 128:(t + 1) * 128, :] if t < 16 else ys[:, :]
        sp.dma(dst, scr[sl][0:rows, :], "scr%d" % sl, reads=[scr_t[sl]])

    return nc, P, sp, locals()


def finish(nc, P, sp):
    for k, v in P.semval.items():
        if v > 0 and not k.startswith("e_"):
            if sp.known.get(k, 0) < v:
                sp.h.wait_ge(P.sems[k], v)
                sp.known[k] = v


def _rel_bucket(dist):
    d = dist.astype(np.int32)
    lr = np.log(np.maximum(d, 1).astype(np.float32) / np.float32(16)) / np.float32(math.log(2048 / 16))
    large = 16 + (lr * np.float32(16)).astype(np.int32)
    large = np.minimum(large, 31)
    return np.where(d < 16, d, large)


def make_consts():
    cfm = np.zeros((128, NCF), np.float32)

    def put(name, arr):
        o, w = CF[name]
        cfm[:, o:o + w] = arr

    put("identf", np.eye(128, dtype=np.float32))
    put("onesf", np.ones((128, 128), np.float32))
    inv = (np.float32(10000.0) ** (-(np.arange(32, dtype=np.float32)) / np.float32(32))).astype(np.float32)
    pos = np.zeros((128, 17), np.float32)
    for c in range(16):
        pos[:, c] = c * 128 + np.arange(128)
    pos[:, 16] = 16384.0
    ang = (pos[:, :, None] * inv[None, None, :]).astype(np.float32)
    put("cos", np.cos(ang).astype(np.float32).reshape(128, 17 * 32))
    put("sin", np.sin(ang).astype(np.float32).reshape(128, 17 * 32))
    lg = np.log1p(-np.exp2(-5.0 - np.arange(8, dtype=np.float64)))
    p1 = (np.arange(128, dtype=np.float64) + 1.0)[:, None]
    put("qdec", np.exp(p1 * lg[None, :]))
    put("kdec", np.exp(-p1 * lg[None, :]) / 8.0)
    put("qdec_s", np.tile(np.exp(lg)[None, :], (128, 1)))
    put("kdec_s", np.full((128, 8), 1.0 / 8.0))
    gam = np.zeros((128, 4))
    for p in range(128):
        for pr in range(4):
            gam[p, pr] = np.exp(128.0 * lg[2 * pr + p // 64])
    put("gamC", gam)
    bd = np.zeros((128, 4))
    bd[0:4, 0:4] = np.eye(4)
    put("bd", bd)
    sel = np.zeros((128, 512))
    for b in range(4):
        sel[b, 128 * b:128 * (b + 1)] = 1.0
    put("sel", sel)
    put("eps", np.full((128, 1), EPS))
    cbm = np.zeros((128, NCB), np.float32)
    cbm[:, 0:128] = np.eye(128)
    cbm[:, 128:256] = 1.0
    jj = np.arange(128)
    cbm[:, 256:384] = (jj[:, None] <= jj[None, :]).astype(np.float32)
    ohg = np.zeros((33, 3 * 383), np.float32)
    ohs = np.zeros((33, 3 * 128), np.float32)
    for g in range(3):
        dil = DILS[g]
        bk = _rel_bucket(np.arange(128) * dil)
        ohg[32, 383 * g:383 * (g + 1)] = NEG
        for j in range(128):
            ohg[bk[j], 383 * g + 127 + j] = 1.0
            ohg[32, 383 * g + 127 + j] = 0.0
        for m in range(128):
            j = 0 if m == 0 else 128 - m
            ohs[bk[j], 128 * g + m] = 1.0
    return cfm, cbm.astype(ml_dtypes.bfloat16), ohg, ohs


def make_in_maps(inp, cores):
    cfm, cbm, ohg, ohs = make_consts()
    f = lambda a: np.ascontiguousarray(np.asarray(a, dtype=np.float32))
    shared = {
        "ln1g": f(inp["ln1_g"]), "w_in": f(inp["w_in"][0]), "gng": f(inp["ret_gn_g"]), "w_rb": f(inp["w_ret_br"][0]),
        "w_ab": f(inp["w_att_br"][0]), "w_o": f(inp["w_out"][0]), "ln2g": f(inp["ln2_g"][0]), "w_up": f(inp["w_up"][0]),
        "w_dn": f(inp["w_down"][0]), "w_pl": f(inp["w_ple"][0]), "w_pg": f(inp["w_ple_gate"][0]),
        "relb": f(inp["rel_bias"]), "lnfg": f(inp["lnf_g"]).reshape(1, D), "cf": cfm, "cb": cbm, "ohg": ohg, "ohs": ohs,
    }
    maps = []
    for c in cores:
        m = dict(shared)
        m["x"] = f(inp["x_prompt"][c])
        m["xs"] = f(inp["x_sample"][4 * c:4 * c + 4, 0])
        m["st"] = f(inp["state_ret"][0, 4 * c:4 * c + 4])
        caches_k = (inp["cache_k_w128"], inp["cache_k_w512"], inp["cache_k_w2048"])
        caches_v = (inp["cache_v_w128"], inp["cache_v_w512"], inp["cache_v_w2048"])
        for g, w in enumerate(WINS):
            m["ck%d" % g] = f(caches_k[g][0, 4 * c:4 * c + 4]).reshape(4, w, 512)
            m["cv%d" % g] = f(caches_v[g][0, 4 * c:4 * c + 4]).reshape(4, w, 512)
        m["pp"] = f(inp["p_prompt"][0, c])
        m["psm"] = f(inp["p_sample"][0, 4 * c:4 * c + 4, 0])
        maps.append(m)
    return maps


_CACHE = {}


def kernel(**inputs):
    if "nc" not in _CACHE:
        nc, P, sp, _ = build_program()
        finish(nc, P, sp)
        _CACHE["nc"] = nc
    nc = _CACHE["nc"]
    maps = make_in_maps(inputs, list(range(8)))
    res = run_bass_kernel_spmd(nc, maps, core_ids=list(range(8)))
    R = res.results
    f32 = np.float32
    cat = lambda name: np.stack([np.asarray(R[c][name]).astype(f32) for c in range(8)], 0)
    y = cat("y")
    ys = np.concatenate([np.asarray(R[c]["ys"]).astype(f32) for c in range(8)], 0).reshape(32, 1, D)
    outs = [y, ys, cat("stp")[None]]
    for g, w in enumerate(WINS):
        outs.append(cat("kp%d" % g).reshape(1, 8, w, 4, 128))
        outs.append(cat("vp%d" % g).reshape(1, 8, w, 4, 128))
    outs.append(np.concatenate([np.asarray(R[c]["sts"]).astype(f32) for c in range(8)], 0)[None])
    for g, w in enumerate(WINS):
        outs.append(np.concatenate([np.asarray(R[c]["ks%d" % g]).astype(f32) for c in range(8)], 0).reshape(1, 32, w, 4, 128))
        outs.append(np.concatenate([np.asarray(R[c]["vs%d" % g]).astype(f32) for c in range(8)], 0).reshape(1, 32, w, 4, 128))
    return tuple(outs)
```

```python
import math
import numpy as np
import ml_dtypes
import concourse.bass as bass
import concourse.mybir as mybir
from concourse.bass_utils import run_bass_kernel_spmd

F32 = mybir.dt.float32
BF16 = mybir.dt.bfloat16
AF = mybir.ActivationFunctionType
ALU = mybir.AluOpType
AX = mybir.AxisListType

S = 2048
NT = 2052
D = 1024
EPS = 1e-6
WINS = (128, 512, 2048)
DILS = (1, 4, 16)
NEG = -30000.0
SAME_ENG_SYNC = True
TB = [(0, 512), (512, 512), (1024, 512), (1536, 512), (2048, 4)]

CF = {}
_o = 0
for _n, _w in (("identf", 128), ("onesf", 128), ("cos", 17 * 32), ("sin", 17 * 32), ("qdec", 8), ("kdec", 8),
               ("qdec_s", 8), ("kdec_s", 8), ("gamC", 4), ("bd", 4), ("sel", 512), ("eps", 1)):
    CF[_n] = (_o, _w)
    _o += _w
NCF = _o
NCB = 384


class Ev:
    __slots__ = ("key", "val", "eng")

    def __init__(self, key, val, eng):
        self.key = key
        self.val = val
        self.eng = eng


class Tile:
    __slots__ = ("name", "w", "r")

    def __init__(self, name):
        self.name = name
        self.w = None
        self.r = {}


def handover(olds, news):
    evs = {}
    for t in olds:
        if t.w is not None:
            evs[("w", t.name)] = t.w
        for k, e in t.r.items():
            evs[(k, t.name)] = e
    for t in news:
        t.r.update(evs)


class Prog:
    def __init__(self, nc):
        self.nc = nc
        self.sems = {}
        self.semval = {}

    def sem(self, key):
        if key not in self.sems:
            self.sems[key] = self.nc.semaphore(key).__enter__()
            self.semval[key] = 0
        return self.sems[key]


class Eng:
    def __init__(self, prog, name, h):
        self.P = prog
        self.name = name
        self.h = h
        self.key = "e_" + name
        prog.sem(self.key)
        self.count = 0
        self.known = {}
        self.pending = []

    def _wait(self, evs):
        need = {}
        for e in evs:
            if e is None:
                continue
            if e.eng is self and (self.name == "pe" or not SAME_ENG_SYNC):
                continue
            if e.val is None:
                raise RuntimeError("dependency on an unsignaled op (%s)" % e.key)
            if need.get(e.key, 0) < e.val:
                need[e.key] = e.val
        for k, v in need.items():
            if self.known.get(k, 0) >= v:
                continue
            self.h.wait_ge(self.P.sems[k], v)
            self.known[k] = v

    @staticmethod
    def _deps(reads, writes):
        evs = []
        for t in reads:
            evs.append(t.w)
        for t in writes:
            evs.append(t.w)
            evs.extend(t.r.values())
        return evs

    def op(self, fn, reads=(), writes=(), signal=True):
        self._wait(self._deps(reads, writes))
        ins = fn(self.h)
        ev = Ev(self.key, None, self)
        if signal:
            self.count += 1
            ins.then_inc(self.P.sems[self.key], 1)
            ev.val = self.count
            for p in self.pending:
                p.val = self.count
            self.pending = []
        else:
            self.pending.append(ev)
        for t in reads:
            t.r[self.name] = ev
        for t in writes:
            t.w = ev
            t.r = {}
        return ev

    def dma(self, out, in_, semkey, reads=(), writes=(), chain=False):
        deps = self._deps(reads, writes)
        if chain:
            deps = [e for e in deps if not (e is not None and e.eng is None and e.key == semkey)]
        self._wait(deps)
        sem = self.P.sem(semkey)
        self.P.semval[semkey] += 16
        self.h.dma_start(out=out, in_=in_).then_inc(sem, 16)
        ev = Ev(semkey, self.P.semval[semkey], None)
        for t in reads:
            t.r["dma_" + semkey] = ev
        for t in writes:
            t.w = ev
            t.r = {}
        return ev


def blk_ap(ap2d, d, blk):
    if d == 1:
        return ap2d[:, blk * 512:(blk + 1) * 512]
    if d == 4:
        return ap2d[:, blk:2048:4]
    return ap2d[:, 0:2048].rearrange("p (l r) -> p r l", r=16)[:, 4 * blk:4 * blk + 4, :]


def tile_ap(ap2d, d, t):
    if d == 1:
        return ap2d[:, t * 128:(t + 1) * 128]
    if d == 4:
        r, a = t // 4, t % 4
        return ap2d[:, r + 512 * a:r + 512 * a + 512:4]
    return ap2d[:, t:2048:16]


def as_blk_shape(ap2d_512, d):
    if d == 16:
        return ap2d_512.rearrange("p (r l) -> p r l", r=4)
    return ap2d_512


def build_program(dbg=False, stop=None):
    nc = bass.Bass("TRN2", target_bir_lowering=False)
    P = Prog(nc)

    def din(name, shape, dt=F32):
        return nc.dram_tensor(name, list(shape), dt, kind="ExternalInput").ap()

    def dout(name, shape, dt=F32):
        return nc.dram_tensor(name, list(shape), dt, kind="ExternalOutput").ap()

    x = din("x", [S, D])
    xs = din("xs", [4, D])
    st_in = din("st", [4, 8, 64, 128])
    ck = [din("ck%d" % g, [4, WINS[g], 512]) for g in range(3)]
    cv = [din("cv%d" % g, [4, WINS[g], 512]) for g in range(3)]
    pp = din("pp", [S, 256])
    psm = din("psm", [4, 256])
    ln1g = din("ln1g", [1, D])
    w_in = din("w_in", [D, 9728])
    gng = din("gng", [1, D])
    w_rb = din("w_rb", [D, D])
    w_ab = din("w_ab", [512, D])
    w_o = din("w_o", [D, D])
    ln2g = din("ln2g", [D])
    w_up = din("w_up", [D, 4096])
    w_dn = din("w_dn", [4096, D])
    w_pl = din("w_pl", [256, D])
    w_pg = din("w_pg", [D, D])
    relb = din("relb", [32, 12])
    lnfg = din("lnfg", [1, D])
    cf_d = din("cf", [128, NCF])
    cb_d = din("cb", [128, NCB], BF16)
    ohg_d = din("ohg", [33, 3 * 383])
    ohs_d = din("ohs", [33, 3 * 128])
    y = dout("y", [S, D])
    ys = dout("ys", [4, D])
    stp = dout("stp", [8, 64, 128])
    kp = [dout("kp%d" % g, [WINS[g], 512]) for g in range(3)]
    vp = [dout("vp%d" % g, [WINS[g], 512]) for g in range(3)]
    sts = dout("sts", [4, 8, 64, 128])
    ks = [dout("ks%d" % g, [4, WINS[g], 512]) for g in range(3)]
    vs = [dout("vs%d" % g, [4, WINS[g], 512]) for g in range(3)]
    tvec = nc.dram_tensor("tvec", [12, 383], F32, kind="Internal").ap()
    tskew = nc.dram_tensor("tskew", [12, 128 * 383], F32, kind="Internal").ap()
    tvec_t = Tile("tvec")
    tskew_ts = [Tile("tskew%d" % i) for i in range(12)]

    pe = Eng(P, "pe", nc.tensor)
    act = Eng(P, "act", nc.scalar)
    dve = Eng(P, "dve", nc.vector)
    pool = Eng(P, "pool", nc.gpsimd)
    sp = Eng(P, "sp", nc.sync)

    def sb(name, shape, dt):
        return nc.sbuf_tensor(name, list(shape), dt).__enter__()

    R1 = sb("R1", [128, 8 * NT], BF16)
    R2 = sb("R2", [128, 4 * NT], F32)
    R4 = sb("R4", [128, 4 * NT], F32)
    R3 = sb("R3", [128, 4 * NT], BF16)
    R5 = sb("R5", [128, 24576], BF16)
    WS = sb("WS", [128, 3, 4096], BF16)
    cf = sb("cf_sb", [128, NCF], F32)
    cb = sb("cb_sb", [128, NCB], BF16)
    gt = sb("gt", [128, D], F32)
    bsl = sb("bsl", [128, 2, 256], F32)
    sm = sb("sm", [128, 256], F32)
    ps = nc.psum_tensor("psum_all", [128, 8, 512], F32).__enter__()

    cf_t, cb_t, gt_t = Tile("cf"), Tile("cb"), Tile("gt")
    bank_t = [Tile("bank%d" % i) for i in range(8)]
    ws_t = [Tile("ws%d" % i) for i in range(3)]
    bsl_t = [Tile("bsl0"), Tile("bsl1")]

    def bank(i):
        return ps[:, i, :]

    def bankbf(i):
        return ps[:, i, :].bitcast(BF16)

    def cfa(name, rows=128):
        o, w = CF[name]
        return cf[0:rows, o:o + w]

    identb = cb[:, 0:128]
    onesb = cb[:, 128:256]
    causal = cb[:, 256:384]
    identf = cfa("identf")
    onesf = cfa("onesf")

    hT = R1[:, :].rearrange("p (k n) -> p k n", k=8)
    hT_t = Tile("hT")
    attT = R3[:, :].rearrange("p (k n) -> p k n", k=4)
    attT_t = Tile("attT")
    R2b = R2[:, :].bitcast(BF16)
    retT = R2b.rearrange("p (k n) -> p k n", k=8)
    retT_t = Tile("retT")
    R5f = R5[:, :].bitcast(F32)

    sp.dma(cf[:, :], cf_d[:, :], "c_cf", writes=[cf_t])
    sp.dma(cb[:, :], cb_d[:, :], "c_cb", writes=[cb_t])

    ws_state = {"n": 0}

    def load_w(src, k_rows, col0, ncols, kchunks, slot=None):
        if slot is None:
            slot = ws_state["n"] % 3
            ws_state["n"] += 1
        view = WS[:, slot, 0:kchunks * ncols].rearrange("p (k n) -> p k n", k=kchunks)
        srcv = src[k_rows:k_rows + kchunks * 128, col0:col0 + ncols].rearrange("(k p) n -> p k n", p=128)
        pool.dma(view, srcv, "ws%d" % slot, writes=[ws_t[slot]])
        return slot, view

    if stop == 'bias':
        return nc, P, sp, locals()
    xt_sl = [R5f[:, 4096 + 1024 * i:5120 + 1024 * i] for i in range(4)]
    xt_t = [Tile("xt%d" % i) for i in range(4)]
    hbs = [R5[:, 0:1024], R5[:, 2048:3072]]
    hb_ts = [Tile("hb0"), Tile("hb1")]
    hb, hb_t = hbs[0], hb_ts[0]
    junks = [R5[:, 1024:2048], R5[:, 3072:4096]]
    junk_ts = [Tile("junk0"), Tile("junk1")]
    junk, junk_t = junks[0], junk_ts[0]
    st_t = Tile("stats")
    st0_t = [Tile("st0a"), Tile("st0b")]
    sp.dma(gt[:, :], ln1g[0:1, :].broadcast_to([128, D]), "c_gt", writes=[gt_t])
    epsc = cfa("eps")

    wq_s, wq_v = load_w(w_in, 0, 3072 + 0, 512, 8)
    wk_s, wk_v = load_w(w_in, 0, 4608 + 0, 512, 8)
    wv_s, wv_v = load_w(w_in, 0, 6144 + 0, 512, 8)
    for t in range(4):
        sp.dma(xt_sl[t][:, :], x[t * 128:(t + 1) * 128, :], "xt%d" % t, writes=[xt_t[t]])
    rbx = sm[0:33, 0:12]
    rbx_t = Tile("rbx")
    oh_t = Tile("oh")
    ohg = R4[0:33, 0:3 * 383]
    ohs = R4[0:33, 1152:1152 + 384]
    dve.op(lambda e: e.memset(sm[0:64, 0:12], 1.0), writes=[rbx_t])
    sp.dma(sm[0:32, 0:12], relb[:, :], "c_rb", writes=[rbx_t])
    sp.dma(ohg, ohg_d[:, :], "c_oh", writes=[oh_t])
    sp.dma(ohs, ohs_d[:, :], "c_oh", writes=[oh_t])
    tv_sb = R4[0:4, 2048:2048 + 3 * 383]
    tv_t = Tile("tv_sb")
    bs_sb = sm[:, 16:28]
    bs_t = Tile("bs_sb")
    for g in range(3):
        pe.op(lambda e, g=g: e.matmul(ps[0:4, 0, 0:383], lhsT=rbx[:, 4 * g:4 * g + 4], rhs=ohg[:, 383 * g:383 * (g + 1)],
                                      start=True, stop=True),
              reads=[rbx_t, oh_t], writes=[bank_t[0]])
        act.op(lambda e, g=g: e.copy(out=tv_sb[:, 383 * g:383 * (g + 1)], in_=ps[0:4, 0, 0:383]),
               reads=[bank_t[0]], writes=[tv_t])
        pe.op(lambda e, g=g: e.matmul(ps[:, 1, 0:4], lhsT=ohs[:, 128 * g:128 * (g + 1)], rhs=rbx[:, 4 * g:4 * g + 4],
                                      start=True, stop=True),
              reads=[rbx_t, oh_t], writes=[bank_t[1]])
        act.op(lambda e, g=g: e.copy(out=bs_sb[:, 4 * g:4 * g + 4], in_=ps[:, 1, 0:4]),
               reads=[bank_t[1]], writes=[bs_t])
    for g in range(3):
        pool.dma(tvec[4 * g:4 * g + 4, :], tv_sb[:, 383 * g:383 * (g + 1)], "c_tv", reads=[tv_t], writes=[tvec_t])
    for gh in range(12):
        pool.dma(tskew[gh, :].rearrange("(c n) -> c n", n=383), tvec[gh:gh + 1, :].broadcast_to([128, 383]),
                 "c_ts%d" % gh, reads=[tvec_t], writes=[tskew_ts[gh]])

    def load_bias(gh, slot):
        src = tskew[gh, 127:127 + 128 * 382].rearrange("(c n) -> c n", n=382)[:, 0:256]
        sp.dma(bsl[:, slot, :], src, "bsl%d" % slot, reads=[tskew_ts[gh]], writes=[bsl_t[slot]])

    def p0_stage1(t):
        rows = 128 if t < 16 else 4
        sl = t % 2
        col = 32 + 4 * sl
        stt_ = st0_t[sl]
        src = x[t * 128:(t + 1) * 128, :] if t < 16 else xs[:, :]
        x4 = t % 4
        if t >= 4:
            sp.dma(xt_sl[x4][0:rows, :], src, "xt%d" % x4, writes=[xt_t[x4]])
        act.op(lambda e: e.activation(out=junks[sl][0:rows, :], in_=xt_sl[x4][0:rows, :], func=AF.Square,
                                      accum_out=sm[0:rows, col:col + 1]),
               reads=[xt_t[x4]], writes=[junk_ts[sl], stt_])
        act.op(lambda e: e.activation(out=sm[0:rows, col + 1:col + 2], in_=sm[0:rows, col:col + 1], func=AF.Sqrt,
                                      bias=epsc[0:rows, :], scale=1.0 / D),
               reads=[stt_, cf_t], writes=[stt_])
        dve.op(lambda e: e.reciprocal(out=sm[0:rows, col + 2:col + 3], in_=sm[0:rows, col + 1:col + 2]),
               reads=[stt_], writes=[stt_])
        dve.op(lambda e: e.scalar_tensor_tensor(
            out=hbs[sl][0:rows, :], in0=xt_sl[x4][0:rows, :], scalar=sm[0:rows, col + 2:col + 3], in1=gt[0:rows, :],
            op0=ALU.mult, op1=ALU.mult), reads=[xt_t[x4], stt_, gt_t], writes=[hb_ts[sl]])

    def p0_stage2(t):
        rows = 128 if t < 16 else 4
        sl = t % 2
        bi = t % 2
        for k in range(8):
            pe.op(lambda e, k=k: e.transpose(
                out=bankbf(bi)[:, k * 128:k * 128 + rows], in_=hbs[sl][0:rows, k * 128:(k + 1) * 128],
                identity=identb[0:rows, 0:rows]),
                reads=[hb_ts[sl], cb_t], writes=[bank_t[bi]], signal=(k == 7))
        act.op(lambda e: e.copy(
            out=hT[:, :, t * 128:t * 128 + rows],
            in_=bankbf(bi).rearrange("p (k c) -> p k c", k=8)[:, :, 0:rows]),
            reads=[bank_t[bi]], writes=[hT_t])

    p0_stage1(0)
    for t in range(17):
        if t + 1 < 17:
            p0_stage1(t + 1)
        p0_stage2(t)

    if dbg:
        dbg_hT = dout("dbg_hT", [128, 8 * NT], BF16)
        sp.dma(dbg_hT[:, :], R1[:, :], "dbg", reads=[hT_t])

    if stop == 'p0':
        return nc, P, sp, locals()
    numacc = R2[:, :].rearrange("p (h n) -> p h n", h=4)
    denacc = R4[:, :].rearrange("p (h n) -> p h n", h=4)
    num_t = [Tile("num%d" % h) for h in range(4)]
    den_t = [Tile("den%d" % h) for h in range(4)]
    vg = R5[:, 0:8192].rearrange("p (t c) -> p t c", t=16)
    vg_t = Tile("vg")
    qk_sl = [R5[:, 8192 + i * 4096:8192 + (i + 1) * 4096] for i in range(2)]
    qT_t = [Tile("qT0"), Tile("qT1")]
    kT_t = [Tile("kT0"), Tile("kT1")]
    ssb = [R5f[:, 8192 + i * 256:8192 + (i + 1) * 256] for i in range(4)]
    ssb_t = [Tile("ssb%d" % i) for i in range(4)]
    ptb = [R5[:, 18432 + i * 256:18432 + (i + 1) * 256] for i in range(4)]
    ptb_t = [Tile("pt%d" % i) for i in range(4)]
    stg = [R5f[:, 9728 + i * 512:9728 + (i + 1) * 512] for i in range(2)]
    stg_t = [Tile("stg%d" % i) for i in range(2)] + [Tile("stg_unused")]
    ckt = [R5f[:, 10752 + i * 512:10752 + (i + 1) * 512] for i in range(2)]
    ckt_t = [Tile("ckt0"), Tile("ckt1")]
    qs_sb = R5f[0:4, 11776:12288]
    qs_t = Tile("qs_sb")
    nums = sm[:, 40:56]
    dens = sm[:, 56:72]
    nds_t = Tile("nds")
    scale = 128.0 ** -0.5
    handover(hb_ts + junk_ts + xt_t, [vg_t, qT_t[0], qT_t[1], kT_t[0], kT_t[1]])
    sslot_t = [Tile("sslot%d" % i) for i in range(4)]
    handover([bank_t[0], bank_t[1]], sslot_t)

    def sslot(i):
        return ps[:, i % 2, 0:256]
    stg_n = {"n": 0}
    blkn = {"n": 0}

    def issue_d2d(g, b):
        W = WINS[g]
        for dst_, src_ in ((ks[g], ck[g]), (vs[g], cv[g])):
            act.dma(dst_[b, 0:W - 1, :].rearrange("w c -> (w c)").rearrange("(a n) -> a n", a=16),
                    src_[b, 1:W, :].rearrange("w c -> (w c)").rearrange("(a n) -> a n", a=16), "d2d")
    d2d_left = []

    load_bias(0, 0)

    def tm_tile(g, which, t, wslot, wview):
        d = DILS[g]
        dst = vp[g] if which == "v" else kp[g]
        if g == 0:
            need, drows = (t == 15), dst[0:128, :]
        elif g == 1:
            need, drows = (t % 4 == 3), dst[(t // 4):512:4, :]
        else:
            need, drows = True, dst[t:2048:16, :]
        if which == "k" and not need:
            return
        bi = 6 + (blkn["n"] % 2)
        blkn["n"] += 1
        for k in range(8):
            pe.op(lambda e, k=k: e.matmul(bank(bi), lhsT=tile_ap(hT[:, k, :], d, t), rhs=wview[:, k, :],
                                          start=(k == 0), stop=(k == 7)),
                  reads=[hT_t, ws_t[wslot]], writes=[bank_t[bi]], signal=(k == 7))
        if which == "v":
            act.op(lambda e: e.copy(out=vg[:, t, :], in_=bank(bi)), reads=[bank_t[bi]], writes=[vg_t])
        if need:
            si = stg_n["n"] % 2
            stg_n["n"] += 1
            act.op(lambda e: e.copy(out=stg[si], in_=bank(bi)), reads=[bank_t[bi]], writes=[stg_t[si]])
            sp.dma(drows, stg[si], "stg%d" % si, reads=[stg_t[si]])

    def proj_group(g, h, which, blk, wslot, wview):
        d = DILS[g]
        qs_i = h % 2
        dstT = qk_sl[qs_i][:, 0:2048] if which == "q" else qk_sl[qs_i][:, 2048:4096]
        dst_t = qT_t[qs_i] if which == "q" else kT_t[qs_i]
        bi = 6 + (blkn["n"] % 2)
        blkn["n"] += 1
        for k in range(8):
            pe.op(lambda e, k=k: e.matmul(bank(bi), lhsT=wview[:, k, 128 * h:128 * (h + 1)],
                                          rhs=hT[:, k, 512 * blk:512 * (blk + 1)], start=(k == 0), stop=(k == 7)),
                  reads=[hT_t, ws_t[wslot]], writes=[bank_t[bi]], signal=(k == 7))
        if d == 1:
            o_ap, i_ap = dstT[:, 512 * blk:512 * (blk + 1)], bank(bi)
        else:
            lw = 512 // d
            o_ap = dstT.rearrange("p (r l) -> p r l", r=d)[:, :, lw * blk:lw * (blk + 1)]
            i_ap = bank(bi).rearrange("p (l r) -> p r l", r=d)
        act.op(lambda e: e.copy(out=o_ap, in_=i_ap), reads=[bank_t[bi]], writes=[dst_t])

    def core(g, h, filler):
        d = DILS[g]
        L = S // d
        nb = L // 128
        gh = 4 * g + h
        qs_i = h % 2
        qT = qk_sl[qs_i][:, 0:2048]
        kT = qk_sl[qs_i][:, 2048:4096]
        bslot = gh % 2
        if gh + 1 < 12:
            load_bias(gh + 1, (gh + 1) % 2)
        Bt = bsl[:, bslot, :]
        KB = [(r, m) for r in range(d) for m in range(nb)]
        LA = 1

        def emit_S(i):
            r, m = KB[i]
            kb = r * L + m * 128
            nq = 256 if m + 1 < nb else 128
            sl4 = i % 4
            sl2 = i % 2
            pe.op(lambda e: e.matmul(sslot(sl2)[:, 0:nq], lhsT=kT[:, kb:kb + 128], rhs=qT[:, kb:kb + nq], start=True, stop=True),
                  reads=[qT_t[qs_i], kT_t[qs_i]], writes=[sslot_t[sl2]])
            dve.op(lambda e: e.scalar_tensor_tensor(out=ssb[sl4][:, 0:nq], in0=sslot(sl2)[:, 0:nq], scalar=scale,
                                                    in1=Bt[:, 0:nq], op0=ALU.mult, op1=ALU.add),
                   reads=[sslot_t[sl2], bsl_t[bslot]], writes=[ssb_t[sl4]])
            act.op(lambda e: e.activation(out=ptb[sl4][:, 0:nq], in_=ssb[sl4][:, 0:nq], func=AF.Exp),
                   reads=[ssb_t[sl4]], writes=[ptb_t[sl4]])

        def emit_PV(i):
            r, m = KB[i]
            kb = r * L + m * 128
            sl4 = i % 4
            qb = kb // 128
            vt = vg[:, qb, 128 * h:128 * (h + 1)]
            ob = 2 + ((qb // 4) % 2)
            dbk = 4 + ((qb // 4) % 2)
            c0 = (qb % 4) * 128
            pe.op(lambda e: e.matmul(ps[:, ob, c0:c0 + 128], lhsT=vt, rhs=ptb[sl4][:, 0:128], start=(m == 0), stop=True),
                  reads=[vg_t, ptb_t[sl4]], writes=[bank_t[ob]], signal=False)
            pe.op(lambda e: e.matmul(ps[:, dbk, c0:c0 + 128], lhsT=onesb, rhs=ptb[sl4][:, 0:128], start=(m == 0), stop=True),
                  reads=[cb_t, ptb_t[sl4]], writes=[bank_t[dbk]], signal=True)
            if (qb % 4) == 3:
                blk = qb // 4
                nv = blk_ap(numacc[:, h, :], d, blk)
                dv = blk_ap(denacc[:, h, :], d, blk)
                if g == 0:
                    act.op(lambda e: e.copy(out=nv, in_=as_blk_shape(bank(ob), d)), reads=[bank_t[ob]], writes=[num_t[h]])
                    act.op(lambda e: e.copy(out=dv, in_=as_blk_shape(bank(dbk), d)), reads=[bank_t[dbk]], writes=[den_t[h]])
                else:
                    dve.op(lambda e: e.tensor_tensor(out=nv, in0=as_blk_shape(bank(ob), d), in1=nv, op=ALU.add),
                           reads=[bank_t[ob], num_t[h]], writes=[num_t[h]])
                    dve.op(lambda e: e.tensor_tensor(out=dv, in0=as_blk_shape(bank(dbk), d), in1=dv, op=ALU.add),
                           reads=[bank_t[dbk], den_t[h]], writes=[den_t[h]])
            if m + 1 < nb:
                qb2 = qb + 1
                ob2 = 2 + ((qb2 // 4) % 2)
                dbk2 = 4 + ((qb2 // 4) % 2)
                c2 = (qb2 % 4) * 128
                pe.op(lambda e: e.matmul(ps[:, ob2, c2:c2 + 128], lhsT=vt, rhs=ptb[sl4][:, 128:256], start=True, stop=False),
                      reads=[vg_t, ptb_t[sl4]], writes=[bank_t[ob2]], signal=False)
                pe.op(lambda e: e.matmul(ps[:, dbk2, c2:c2 + 128], lhsT=onesb, rhs=ptb[sl4][:, 128:256], start=True, stop=False),
                      reads=[cb_t, ptb_t[sl4]], writes=[bank_t[dbk2]], signal=True)

        for i in range(LA):
            emit_S(i)
        for i in range(16):
            if i + LA < 16:
                emit_S(i + LA)
            if i % 2 == 1 and filler:
                filler.pop(0)()
            emit_PV(i)
        while filler:
            filler.pop(0)()

    for g in range(3):
        d = DILS[g]
        W = WINS[g]
        tm_work = [(lambda t=t, a=wv_s, b_=wv_v: tm_tile(g, "v", t, a, b_)) for t in range(16)]
        if g < 2:
            tm_work += [(lambda t=t, a=wk_s, b_=wk_v: tm_tile(g, "k", t, a, b_)) for t in range(16)]
        tm_per_b = (len(tm_work) + 3) // 4
        svals = {}
        for wi, (wslot, wview) in enumerate(((wq_s, wq_v), (wk_s, wk_v), (wv_s, wv_v))):
            bi = 6 + (blkn["n"] % 2)
            blkn["n"] += 1
            for k in range(8):
                pe.op(lambda e, k=k, bi=bi, wview=wview: e.matmul(
                    ps[0:4, bi, :], lhsT=hT[:, k, 2048:2052], rhs=wview[:, k, :], start=(k == 0), stop=(k == 7)),
                    reads=[hT_t, ws_t[wslot]], writes=[bank_t[bi]], signal=(k == 7))
            if wi == 0:
                act.op(lambda e, bi=bi: e.copy(out=qs_sb, in_=ps[0:4, bi, :]), reads=[bank_t[bi]], writes=[qs_t])
            else:
                nrow = gt[0:4, 512 * (wi - 1):512 * wi]
                act.op(lambda e, bi=bi, nrow=nrow: e.copy(out=nrow, in_=ps[0:4, bi, :]),
                       reads=[bank_t[bi]], writes=[gt_t])
                dst = ks[g] if wi == 1 else vs[g]
                sp.dma(dst[:, W - 1, :], nrow, "c_nr", reads=[gt_t])
                svals[wi] = nrow
        for b in range(4):
            for wi, cache in ((1, ck[g]), (2, cv[g])):
                ci = wi - 1
                sp.dma(ckt[ci][:, :], cache[b, 0:W:d, :], "ckt%d" % ci, writes=[ckt_t[ci]])
                sp.dma(ckt[ci][0:1, :], svals[wi][b:b + 1, :], "ckt%d" % ci, reads=[gt_t],
                       writes=[ckt_t[ci]])
            bq = 6 + (blkn["n"] % 2)
            blkn["n"] += 1
            o_sel = CF["sel"][0]
            pe.op(lambda e, b=b, bq=bq: e.matmul(bank(bq), lhsT=cf[0:4, o_sel + 128 * b:o_sel + 128 * (b + 1)],
                                                 rhs=qs_sb, start=True, stop=True),
                  reads=[cf_t, qs_t], writes=[bank_t[bq]])
            prodv = R5f[:, 8192:8704]
            dve.op(lambda e, bq=bq: e.tensor_tensor(out=prodv, in0=bank(bq), in1=ckt[0][:, :], op=ALU.mult),
                   reads=[bank_t[bq], ckt_t[0]], writes=[ssb_t[0], ssb_t[1]])
            dve.op(lambda e: e.tensor_reduce(out=sm[:, 80:84], in_=prodv.rearrange("p (h e) -> p h e", h=4),
                                             axis=AX.X, op=ALU.add),
                   reads=[ssb_t[0], ssb_t[1]], writes=[st_t])
            dve.op(lambda e, g=g: e.scalar_tensor_tensor(out=sm[:, 84:88], in0=sm[:, 80:84], scalar=scale,
                                                         in1=bs_sb[:, 4 * g:4 * g + 4], op0=ALU.mult, op1=ALU.add),
                   reads=[st_t, bs_t], writes=[st_t])
            act.op(lambda e: e.activation(out=sm[:, 88:92], in_=sm[:, 84:88], func=AF.Exp),
                   reads=[st_t], writes=[st_t])
            for _ in range(tm_per_b):
                if tm_work:
                    tm_work.pop(0)()
            bo = 4 + (b % 2)
            for h in range(4):
                pe.op(lambda e, h=h, bo=bo: e.matmul(ps[:, bo, 8 + 4 * h:12 + 4 * h], lhsT=ckt[1][:, 128 * h:128 * (h + 1)],
                                                     rhs=sm[:, 88:92], start=True, stop=True),
                      reads=[ckt_t[1], st_t], writes=[bank_t[bo]], signal=False)
            pe.op(lambda e, bo=bo: e.matmul(ps[:, bo, 4:8], lhsT=onesf, rhs=sm[:, 88:92], start=True, stop=True),
                  reads=[cf_t, st_t], writes=[bank_t[bo]])
            if g == 0:
                act.op(lambda e, b=b, bo=bo: e.copy(out=nums[:, 4 * b:4 * b + 4], in_=ps[:, bo, 8:24:5]),
                       reads=[bank_t[bo]], writes=[nds_t])
                act.op(lambda e, b=b, bo=bo: e.copy(out=dens[:, 4 * b:4 * b + 4], in_=ps[:, bo, 4:8]),
                       reads=[bank_t[bo]], writes=[nds_t])
            else:
                dve.op(lambda e, b=b, bo=bo: e.tensor_tensor(out=nums[:, 4 * b:4 * b + 4], in0=ps[:, bo, 8:24:5],
                                                             in1=nums[:, 4 * b:4 * b + 4], op=ALU.add),
                       reads=[bank_t[bo], nds_t], writes=[nds_t])
                dve.op(lambda e, b=b, bo=bo: e.tensor_tensor(out=dens[:, 4 * b:4 * b + 4], in0=ps[:, bo, 4:8],
                                                             in1=dens[:, 4 * b:4 * b + 4], op=ALU.add),
                       reads=[bank_t[bo], nds_t], writes=[nds_t])

        while tm_work:
            tm_work.pop(0)()
        if g + 1 < 3:
            wv_s, wv_v = load_w(w_in, 0, 6144 + 512 * (g + 1), 512, 8, slot=wv_s)
        if g == 0:
            for which, (wsl, wvw) in (("q", (wq_s, wq_v)), ("k", (wk_s, wk_v))):
                for blk in range(4):
                    proj_group(g, 0, which, blk, wsl, wvw)
        for h in range(4):
            if h < 3:
                gn_, hn_ = g, h + 1
            elif g + 1 < 3:
                gn_, hn_ = g + 1, 0
                wq_s, wq_v = load_w(w_in, 0, 3072 + 512 * (g + 1), 512, 8, slot=wq_s)
                wk_s, wk_v = load_w(w_in, 0, 4608 + 512 * (g + 1), 512, 8, slot=wk_s)
            else:
                gn_, hn_ = None, None
            if gn_ is None:
                filler = [(lambda t=t, a=wk_s, b_=wk_v: tm_tile(2, "k", t, a, b_)) for t in range(16)]
            else:
                filler = [(lambda which=which, blk=blk, wsl=wsl, wvw=wvw, gn_=gn_, hn_=hn_: proj_group(gn_, hn_, which, blk, wsl, wvw))
                          for which, (wsl, wvw) in (("q", (wq_s, wq_v)), ("k", (wk_s, wk_v))) for blk in range(4)]
            if d2d_left:
                issue_d2d(2, d2d_left.pop(0))
            core(g, h, filler)
            if stop == 'a_c%d' % (4 * g + h):
                return nc, P, sp, locals()
    handover(sslot_t, [bank_t[0], bank_t[1]])

    WR_MAP = {0: 4, 1: 5, 2: 2, 3: 3, 4: 0, 5: 1}

    def wr_blk(j):
        o = WR_MAP[j] * 4096
        return R5[:, o:o + 4096].rearrange("p (k n) -> p k n", k=8)
    wr_ts = [Tile("wr%d" % j) for j in range(6)]
    handover([ssb_t[0], ssb_t[1], ssb_t[2], ssb_t[3], ptb_t[0], ptb_t[1], ptb_t[2], ptb_t[3], stg_t[0], stg_t[1],
              ckt_t[0], ckt_t[1], qs_t], [wr_ts[0], wr_ts[1]])
    handover([qT_t[0], qT_t[1], kT_t[0], kT_t[1]], [wr_ts[2], wr_ts[3]])
    for j in range(4):
        pool.dma(wr_blk(j), w_in[0:1024, 512 * j:512 * (j + 1)].rearrange("(k p) n -> p k n", p=128), "wr%d" % j,
                 writes=[wr_ts[j]])
    rtmp = R5f[:, 0:2048]
    rtmp_t = Tile("rtmp")
    handover([vg_t, qT_t[0], qT_t[1], kT_t[0], kT_t[1]], [rtmp_t])
    rtmps = [R5f[:, 0:2048], R5f[:, 2048:4096]]
    rtmp_ts = [rtmp_t, Tile("rtmp1")]
    handover([vg_t, qT_t[0], qT_t[1], kT_t[0], kT_t[1]], [rtmp_ts[1]])
    for h in range(4):
        rt_, rt_t_ = rtmps[h % 2], rtmp_ts[h % 2]
        act.op(lambda e, h=h, rt_=rt_: e.activation(out=rt_, in_=denacc[:, h, 0:2048], func=AF.Ln),
               reads=[den_t[h]], writes=[rt_t_])
        act.op(lambda e, rt_=rt_: e.activation(out=rt_, in_=rt_, func=AF.Exp, scale=-1.0), reads=[rt_t_], writes=[rt_t_])
        dve.op(lambda e, h=h, rt_=rt_: e.tensor_tensor(out=attT[:, h, 0:2048], in0=numacc[:, h, 0:2048], in1=rt_, op=ALU.mult),
               reads=[num_t[h], rt_t_], writes=[attT_t])
    dve.op(lambda e: e.reciprocal(out=sm[:, 96:112], in_=dens), reads=[nds_t], writes=[st_t])
    dve.op(lambda e: e.tensor_tensor(out=attT[:, :, 2048:2052],
                                     in0=nums.rearrange("p (b h) -> p h b", b=4),
                                     in1=sm[:, 96:112].rearrange("p (b h) -> p h b", b=4), op=ALU.mult),
           reads=[nds_t, st_t], writes=[attT_t])

    if dbg:
        dbg_att = dout("dbg_att", [128, 4 * NT], BF16)
        sp.dma(dbg_att[:, :], R3[:, :], "dbg", reads=[attT_t])


    if stop == 'att':
        return nc, P, sp, locals()
    handover([vg_t, rtmp_t, rtmp_ts[1]], [wr_ts[4], wr_ts[5]])
    for j in (4, 5):
        pool.dma(wr_blk(j), w_in[0:1024, 512 * j:512 * (j + 1)].rearrange("(k p) n -> p k n", p=128), "wr%d" % j,
                 writes=[wr_ts[j]])
    R4b = R4[:, :].bitcast(BF16)
    qf, kf = R4[:, 0:512], R4[:, 512:1024]
    tt = [R4[:, 1024 + 256 * i:1024 + 256 * i + 512] for i in range(3)] + [R4[:, 1792:2048]]
    tall = R4[:, 1024:2048]
    gs, on, S32, stt = R4[:, 2048:3072], R4[:, 3072:4096], R4[:, 4096:4608], R4[:, 4608:4672]
    qd, kd, vb = R4b[:, 9344:9856], R4b[:, 9856:10368], R4b[:, 10368:11392]
    qkT = R4b[:, 11392:12416]
    qdT = R4b[:, 11392:11904].rearrange("p (a n) -> p a n", a=4)
    kdT = R4b[:, 11904:12416].rearrange("p (a n) -> p a n", a=4)
    mT = R4b[:, 12416:13440].rearrange("p (h n) -> p h n", h=8)
    ro = R4b[:, 13440:14464]
    Sb = R4b[:, 14464:14976].rearrange("p (a n) -> p a n", a=4)
    rt_names = ["qf", "kf", "tt", "gs", "on", "S32", "stt", "qd", "kd", "vb", "qkT", "mT", "ro", "Sb", "ttk"]
    RT = {n: Tile("r_" + n) for n in rt_names}
    handover(den_t + [tv_t, oh_t], list(RT.values()))
    handover(num_t, [retT_t])
    ttk = [R5f[:, 0:0]]
    WSf = WS[:, :, :].rearrange("p s n -> p (s n)").bitcast(F32)
    tk = [WSf[:, 4096 + 256 * i:4096 + 256 * i + 512] for i in range(3)] + [WSf[:, 4864:5120]]
    sp.dma(gt[:, :], gng[0:1, :].broadcast_to([128, D]), "c_gt", writes=[gt_t])
    dve.op(lambda e: e.memset(S32, 0.0), writes=[RT["S32"]])
    o_cos, o_sin = CF["cos"][0], CF["sin"][0]
    S_s = WSf[0:64, 0:4096]
    Ss4 = S_s.rearrange("p (b h v) -> p b h v", b=4, h=8)
    Vbd = WS[0:4, 2, :].rearrange("p (h b v) -> p h b v", h=8, b=4)
    Ss_t = Tile("S_s")
    Vbd_t = Tile("Vbd")

    def bc_heads(tab, rows):
        return tab.unsqueeze(1).broadcast_to([rows, 8, 32])

    def rotary(eng, src, tmps, c, rows, tile_src, tile_tmp):
        X = src[0:rows, :].rearrange("p (h t f) -> p h t f", h=8, t=2)
        x1, x2 = X[:, :, 0, :], X[:, :, 1, :]
        ct = cf[0:rows, o_cos + 32 * c:o_cos + 32 * (c + 1)]
        st_ = cf[0:rows, o_sin + 32 * c:o_sin + 32 * (c + 1)]
        cs4 = ct.unsqueeze(1).unsqueeze(1).broadcast_to([rows, 8, 2, 32])
        sn = bc_heads(st_, rows)
        t1 = tmps[0][0:rows, :]
        t2 = tmps[2][0:rows, :]
        T1 = t1.rearrange("p (h t f) -> p h t f", h=8, t=2)
        T2 = t2.rearrange("p (h t f) -> p h t f", h=8, t=2)
        eng.op(lambda e: e.tensor_tensor(out=T1, in0=X, in1=cs4, op=ALU.mult), reads=[tile_src, cf_t], writes=[tile_tmp])
        eng.op(lambda e: e.tensor_tensor(out=T2[:, :, 0, :], in0=x2, in1=sn, op=ALU.mult), reads=[tile_src, cf_t], writes=[tile_tmp])
        eng.op(lambda e: e.tensor_tensor(out=T2[:, :, 1, :], in0=x1, in1=sn, op=ALU.mult), reads=[tile_src, cf_t], writes=[tile_tmp])
        eng.op(lambda e: e.tensor_tensor(out=x1, in0=T1[:, :, 0, :], in1=T2[:, :, 0, :], op=ALU.subtract),
               reads=[tile_tmp], writes=[tile_src])
        eng.op(lambda e: e.tensor_tensor(out=x2, in0=T1[:, :, 1, :], in1=T2[:, :, 1, :], op=ALU.add),
               reads=[tile_tmp], writes=[tile_src])

    def dec_bc(name, rows):
        o, w = CF[name]
        return cf[0:rows, o:o + 8].unsqueeze(2).broadcast_to([rows, 8, 64])

    WSb = WS[:, :, :].rearrange("p s n -> p (s n)")
    def gsi(c):
        return 0 if c == 16 else c % 3
    gs3 = [gs, WSf[:, 0:1024], WSf[:, 1024:2048]]
    gs3_t = [RT["gs"], Tile("gs_b"), Tile("gs_c")]
    SA = [
        dict(qd=qd, kd=kd, vb=vb, qkT=qkT, gs=gs),
        dict(qd=WSb[:, 4096:4608], kd=WSb[:, 4608:5120], vb=WSb[:, 5120:6144], qkT=WSb[:, 6144:7168], gs=WSf[:, 0:1024]),
    ]
    SAT = [dict(qd=RT["qd"], kd=RT["kd"], vb=RT["vb"], qkT=RT["qkT"], gs=RT["gs"]),
           {n: Tile("s1_" + n) for n in ("qd", "kd", "vb", "qkT", "gs")}]
    tallB_t = gs3_t[2]
    handover([ws_t[0], ws_t[1]], list(SAT[1].values()) + [gs3_t[1], gs3_t[2]])
    o_g = CF["gamC"][0]
    o_bd = CF["bd"][0]
    lgam = [math.exp(math.log1p(-2.0 ** (-5 - h))) for h in range(8)]

    def stage_A(c, part):
        rows = 128 if c < 16 else 4
        tok = slice(c * 128, c * 128 + rows)
        s_ = c % 2
        B_, T_ = SA[s_], SAT[s_]
        abank = [0, 1, 2, 0, 1, 2]
        for j in (range(4) if part == 0 else range(4, 6)):
            bj = abank[j]
            for k in range(8):
                pe.op(lambda e, j=j, k=k: e.matmul(ps[0:rows, bj, :], lhsT=hT[:, k, tok], rhs=wr_blk(j)[:, k, :],
                                                   start=(k == 0), stop=(k == 7)),
                      reads=[hT_t, wr_ts[j]], writes=[bank_t[bj]], signal=(k == 7))
            if j == 0:
                act.op(lambda e: e.copy(out=qf[0:rows, :], in_=ps[0:rows, bj, :]), reads=[bank_t[bj]], writes=[RT["qf"]])
            elif j == 1:
                act.op(lambda e: e.copy(out=kf[0:rows, :], in_=ps[0:rows, bj, :]), reads=[bank_t[bj]], writes=[RT["kf"]])
            elif j in (2, 3):
                act.op(lambda e, j=j: e.copy(out=B_["vb"][0:rows, 512 * (j - 2):512 * (j - 1)], in_=ps[0:rows, bj, :]),
                       reads=[bank_t[bj]], writes=[T_["vb"]])
            else:
                act.op(lambda e, j=j: e.activation(out=gs3[gsi(c)][0:rows, 512 * (j - 4):512 * (j - 3)], in_=ps[0:rows, bj, :],
                                                   func=AF.Silu), reads=[bank_t[bj]], writes=[gs3_t[gsi(c)]])
        if part == 0:
            rotary(dve, qf, tt, c, rows, RT["qf"], RT["tt"])
            rotary(pool, kf, tk, c, rows, RT["kf"], ws_t[2])
        if c < 16 and part == 0:
            dve.op(lambda e: e.tensor_tensor(out=B_["qd"].rearrange("p (h f) -> p h f", h=8),
                                             in0=qf.rearrange("p (h f) -> p h f", h=8), in1=dec_bc("qdec", 128), op=ALU.mult),
                   reads=[RT["qf"], cf_t], writes=[T_["qd"]])
            pool.op(lambda e: e.tensor_tensor(out=B_["kd"].rearrange("p (h f) -> p h f", h=8),
                                              in0=kf.rearrange("p (h f) -> p h f", h=8), in1=dec_bc("kdec", 128), op=ALU.mult),
                    reads=[RT["kf"], cf_t], writes=[T_["kd"]])
        if part == 0:
            return
        if c < 16:
            for i, srcb in enumerate((B_["qd"], B_["kd"])):
                for pr in range(4):
                    pe.op(lambda e, i=i, pr=pr, srcb=srcb: e.transpose(
                        out=bankbf(0)[:, 512 * i + 128 * pr:512 * i + 128 * (pr + 1)], in_=srcb[:, 128 * pr:128 * (pr + 1)],
                        identity=identb), reads=[T_["qd"], T_["kd"], cb_t], writes=[bank_t[0]], signal=(i == 1 and pr == 3))
            act.op(lambda e: e.copy(out=B_["qkT"], in_=bankbf(0)), reads=[bank_t[0]], writes=[T_["qkT"]])
        pool.op(lambda e: e.tensor_tensor(out=gs3[gsi(c)][0:rows, :], in0=gs3[gsi(c)][0:rows, :], in1=gt[0:rows, :], op=ALU.mult),
                reads=[gs3_t[gsi(c)], gt_t], writes=[gs3_t[gsi(c)]])

    def stage_B(c):
        rows = 128 if c < 16 else 4
        tok = slice(c * 128, c * 128 + rows)
        s_ = c % 2
        B_, T_ = SA[s_], SAT[s_]
        vb_, kd_ = B_["vb"], B_["kd"]
        qdT_ = B_["qkT"][:, 0:512].rearrange("p (a n) -> p a n", a=4)
        kdT_ = B_["qkT"][:, 512:1024].rearrange("p (a n) -> p a n", a=4)
        if c < 16:
            for h in range(8):
                base, pr = (h % 2) * 64, h // 2
                pe.op(lambda e, h=h, base=base, pr=pr: e.matmul(
                    ps[:, 3 + h % 2, 128 * pr:128 * (pr + 1)], lhsT=kdT_[base:base + 64, pr, :], rhs=qdT_[base:base + 64, pr, :],
                    start=True, stop=True), reads=[T_["qkT"]], writes=[bank_t[3 + h % 2]], signal=(h >= 6))
            for par in range(2):
                dve.op(lambda e, par=par: e.tensor_tensor(
                    out=mT[:, par::2, :], in0=bank(3 + par).rearrange("p (a n) -> p a n", a=4),
                    in1=causal.unsqueeze(1).broadcast_to([128, 4, 128]), op=ALU.mult),
                    reads=[bank_t[3 + par], cb_t], writes=[RT["mT"]])
            for h in range(8):
                base, pr = (h % 2) * 64, h // 2
                ob_ = 5 + h // 4
                oc = (h % 4) * 128
                pe.op(lambda e, h=h, ob_=ob_, oc=oc: e.matmul(
                    ps[:, ob_, oc:oc + 128], lhsT=mT[:, h, :], rhs=vb_[:, 128 * h:128 * (h + 1)], start=True, stop=(c == 0)),
                    reads=[RT["mT"], T_["vb"]], writes=[bank_t[ob_]], signal=(c == 0 and h % 4 == 3))
                if c > 0:
                    pe.op(lambda e, h=h, ob_=ob_, oc=oc, base=base, pr=pr: e.matmul(
                        ps[:, ob_, oc:oc + 128], lhsT=qdT_[:, pr, :], rhs=Sbz[:, h, :],
                        start=False, stop=True), reads=[T_["qkT"], RT["Sb"]], writes=[bank_t[ob_]], signal=(h % 4 == 3))
            for h in range(8):
                base, pr = (h % 2) * 64, h // 2
                pe.op(lambda e, h=h, base=base, pr=pr: e.matmul(
                    ps[base:base + 64, 7, 128 * pr:128 * (pr + 1)], lhsT=kd_[:, 64 * h:64 * (h + 1)],
                    rhs=vb_[:, 128 * h:128 * (h + 1)], start=True, stop=True),
                    reads=[T_["kd"], T_["vb"]], writes=[bank_t[7]], signal=(h == 7))
            return

    def stage_Bb(c):
        if True:
            act.op(lambda e: e.copy(out=on, in_=ps[:, 5:7, :].rearrange("p a n -> p (a n)")),
                   reads=[bank_t[5], bank_t[6]], writes=[RT["on"]])
            dve.op(lambda e: e.tensor_tensor(out=S32, in0=bank(7), in1=S32, op=ALU.add),
                   reads=[bank_t[7], RT["S32"]], writes=[RT["S32"]])
            dve.op(lambda e: e.tensor_tensor(out=S32.rearrange("p (a n) -> p a n", a=4),
                                             in0=S32.rearrange("p (a n) -> p a n", a=4),
                                             in1=cf[:, o_g:o_g + 4].unsqueeze(2).broadcast_to([128, 4, 128]), op=ALU.mult),
                   reads=[RT["S32"], cf_t], writes=[RT["S32"]])
            for e2 in range(2):
                act.op(lambda e, e2=e2: e.copy(out=Sbz[64 * e2:64 * (e2 + 1), e2::2, :],
                                               in_=S32[64 * e2:64 * (e2 + 1), :].rearrange("p (a n) -> p a n", a=4)),
                       reads=[RT["S32"]], writes=[RT["Sb"]])

    def stage_Bs(c):
        rows = 4
        tok = slice(c * 128, c * 128 + rows)
        s_ = c % 2
        B_, T_ = SA[s_], SAT[s_]
        vb_, kd_ = B_["vb"], B_["kd"]
        if True:
            pool.op(lambda e: e.tensor_tensor(out=kd_[0:4, :].rearrange("p (h f) -> p h f", h=8),
                                              in0=kf[0:4, :].rearrange("p (h f) -> p h f", h=8), in1=dec_bc("kdec_s", 4),
                                              op=ALU.mult), reads=[RT["kf"], cf_t], writes=[T_["kd"]])
            for b in range(4):
                dve.op(lambda e, b=b: e.tensor_scalar(out=Vbd[:, :, b, :], in0=vb_[0:4, :].rearrange("p (h v) -> p h v", h=8),
                                                      scalar1=cf[0:4, o_bd + b:o_bd + b + 1], scalar2=None, op0=ALU.mult),
                       reads=[T_["vb"], cf_t], writes=[Vbd_t, ws_t[2], RT["Sb"]])
            for h in range(8):
                bk_ = 3 + h % 2
                pe.op(lambda e, h=h, bk_=bk_: e.matmul(ps[0:64, bk_, :], lhsT=kd_[0:4, 64 * h:64 * (h + 1)],
                                                       rhs=Vbd[:, h, :, :], start=True, stop=True),
                      reads=[T_["kd"], Vbd_t], writes=[bank_t[bk_]])
                dve.op(lambda e, h=h, bk_=bk_: e.scalar_tensor_tensor(
                    out=Ss4[:, :, h, :], in0=Ss4[:, :, h, :], scalar=lgam[h],
                    in1=ps[0:64, bk_, :].rearrange("p (b v) -> p b v", b=4), op0=ALU.mult, op1=ALU.add),
                    reads=[bank_t[bk_], Ss_t], writes=[Ss_t])
            sp.dma(sts.rearrange("b h d v -> d (b h) v"), S_s.rearrange("p (m v) -> p m v", v=128), "c_so", reads=[Ss_t])
            for h in range(8):
                pe.op(lambda e, h=h: e.transpose(out=ps[0:64, 7, 4 * h:4 * h + 4], in_=qf[0:4, 64 * h:64 * (h + 1)],
                                                 identity=identf[0:4, 0:4]),
                      reads=[RT["qf"], cf_t], writes=[bank_t[7]], signal=(h == 7))
            qsT = stt[0:64, 0:32]
            act.op(lambda e: e.copy(out=qsT, in_=ps[0:64, 7, 0:32]), reads=[bank_t[7]], writes=[RT["stt"]])
            for h in range(8):
                bk_ = 5 + h % 2
                pe.op(lambda e, h=h, bk_=bk_: e.matmul(ps[0:4, bk_, :], lhsT=qsT[:, 4 * h:4 * h + 4], rhs=Ss4[:, :, h, :],
                                                       start=True, stop=True),
                      reads=[RT["stt"], Ss_t], writes=[bank_t[bk_]])
                dve.op(lambda e, bk_=bk_: e.tensor_tensor(
                    out=tall[0:4, 0:512].rearrange("p (b v) -> p b v", b=4),
                    in0=ps[0:4, bk_, :].rearrange("p (b v) -> p b v", b=4),
                    in1=cf[0:4, o_bd:o_bd + 4].unsqueeze(2).broadcast_to([4, 4, 128]), op=ALU.mult),
                    reads=[bank_t[bk_], cf_t], writes=[RT["tt"]])
                dve.op(lambda e, h=h: e.tensor_reduce(out=on[0:4, 128 * h:128 * (h + 1)],
                                                      in_=tall[0:4, 0:512].rearrange("p (b v) -> p v b", b=4),
                                                      axis=AX.X, op=ALU.add), reads=[RT["tt"]], writes=[RT["on"]])

    def stage_B2(c):
        rows = 128 if c < 16 else 4
        s_ = c % 2
        B_, T_ = SA[s_], SAT[s_]
        ro, ro_t = ros[c % 2], ro_ts[c % 2]
        sq_tmp, sq_t = tall, RT["tt"]
        on3 = on[0:rows, :].rearrange("p (h v) -> p h v", h=8)
        sA, sB, sC, sD = tuple(stt[0:rows, 32 + 8 * i:40 + 8 * i] for i in range(4))
        dve.op(lambda e: e.tensor_reduce(out=sA, in_=on3, axis=AX.X, op=ALU.add), reads=[RT["on"]], writes=[RT["stt"]])
        act.op(lambda e: e.activation(out=sq_tmp[0:rows, :], in_=on[0:rows, :], func=AF.Square),
               reads=[RT["on"]], writes=[sq_t])
        dve.op(lambda e: e.tensor_reduce(out=sB, in_=sq_tmp[0:rows, :].rearrange("p (h v) -> p h v", h=8), axis=AX.X,
                                         op=ALU.add), reads=[sq_t], writes=[RT["stt"]])
        dve.op(lambda e: e.tensor_scalar(out=sA, in0=sA, scalar1=1.0 / 128, scalar2=None, op0=ALU.mult),
               reads=[RT["stt"]], writes=[RT["stt"]])
        dve.op(lambda e: e.tensor_tensor(out=sC, in0=sA, in1=sA, op=ALU.mult), reads=[RT["stt"]], writes=[RT["stt"]])
        dve.op(lambda e: e.scalar_tensor_tensor(out=sB, in0=sB, scalar=1.0 / 128, in1=sC, op0=ALU.mult, op1=ALU.subtract),
               reads=[RT["stt"]], writes=[RT["stt"]])
        act.op(lambda e: e.activation(out=sC, in_=sB, func=AF.Sqrt, bias=epsc[0:rows, :], scale=1.0),
               reads=[RT["stt"], cf_t], writes=[RT["stt"]])
        dve.op(lambda e: e.reciprocal(out=sD, in_=sC), reads=[RT["stt"]], writes=[RT["stt"]])
        dve.op(lambda e: e.tensor_tensor(out=on3, in0=on3, in1=sA.unsqueeze(2).broadcast_to([rows, 8, 128]), op=ALU.subtract),
               reads=[RT["on"], RT["stt"]], writes=[RT["on"]])
        dve.op(lambda e: e.tensor_tensor(out=on3, in0=on3, in1=sD.unsqueeze(2).broadcast_to([rows, 8, 128]), op=ALU.mult),
               reads=[RT["on"], RT["stt"]], writes=[RT["on"]])
        pool.op(lambda e: e.tensor_tensor(out=ro[0:rows, :], in0=on[0:rows, :], in1=gs3[gsi(c)][0:rows, :], op=ALU.mult),
                reads=[RT["on"], gs3_t[gsi(c)]], writes=[ro_t])

    def stage_C(c):
        rows = 128 if c < 16 else 4
        tok = slice(c * 128, c * 128 + rows)
        ro, ro_t = ros[c % 2], ro_ts[c % 2]
        for h in range(8):
            pe.op(lambda e, h=h: e.transpose(out=bankbf(3)[:, 128 * h:128 * h + rows], in_=ro[0:rows, 128 * h:128 * (h + 1)],
                                             identity=identb[0:rows, 0:rows]),
                  reads=[ro_t, cb_t], writes=[bank_t[3]], signal=(h == 7))
        act.op(lambda e: e.copy(out=retT[:, :, tok], in_=bankbf(3).rearrange("p (k c) -> p k c", k=8)[:, :, 0:rows]),
               reads=[bank_t[3]], writes=[retT_t])
        if c == 15:
            for hh in range(2):
                sp.dma(stp.rearrange("(pr two) d v -> two d pr v", two=2)[hh],
                       S32[64 * hh:64 * (hh + 1), :].rearrange("p (a n) -> p a n", a=4), "c_sp%d" % hh, reads=[RT["S32"]])

    Sbz = WSb[:, 10240:11264].rearrange("p (h n) -> p h n", h=8)
    dve.op(lambda e: e.memset(WSb[:, 10240:11264], 0.0), writes=[RT["Sb"], ws_t[2]])
    ros = [ro, R4b[:, 14976:16000]]
    ro_ts = [RT["ro"], Tile("ro2")]
    handover(den_t + [tv_t, oh_t], [ro_ts[1]])
    wab = R5[:, 19488:23584].rearrange("p (k n) -> p k n", k=4)
    wab_t = Tile("wab")
    t1s0_t = Tile("t1s0")
    t1s0 = [R5[:, 16416 + wi * 1024:16416 + (wi + 1) * 1024].rearrange("p (k n) -> p k n", k=8) for wi in range(3)]
    stage_A(0, 0)
    stage_A(0, 1)
    for c in range(16):
        if c + 1 < 16:
            stage_A(c + 1, 0)
            stage_A(c + 1, 1)
        stage_B(c)
        if c >= 1:
            stage_B2(c - 1)
        stage_Bb(c)
        if c >= 2:
            stage_C(c - 2)
    stage_B2(15)
    stage_C(14)
    sp.dma(S_s.rearrange("p (m v) -> p m v", v=128), st_in.rearrange("b h d v -> d (b h) v"), "c_ss",
           writes=[Ss_t, ws_t[0], ws_t[1], gs3_t[1], gs3_t[2]] + list(SAT[1].values()))
    stage_A(16, 0)
    stage_A(16, 1)
    handover(wr_ts, [wab_t, t1s0_t])
    pool.dma(wab, w_ab[:, :].rearrange("(k p) n -> p k n", p=128), "wab", writes=[wab_t])
    for wi, (src_, c0_) in enumerate(((w_rb, 0), (w_in, 7680), (w_in, 8704))):
        pool.dma(t1s0[wi], src_[0:1024, c0_:c0_ + 128].rearrange("(k p) n -> p k n", p=128), "t1s0", writes=[t1s0_t],
                 chain=(wi > 0))
    stage_Bs(16)
    stage_B2(16)
    stage_C(15)
    stage_C(16)
    if dbg:
        dbg_ret = dout("dbg_ret", [128, 8 * NT], BF16)
        sp.dma(dbg_ret[:, :], R2b, "dbg", reads=[retT_t])
    if stop == 'ret':
        return nc, P, sp, locals()


    mixedT = R5[:, 0:8 * NT].rearrange("p (k n) -> p k n", k=8)
    mixed_t = Tile("mixedT")
    handover(wr_ts, [mixed_t])
    T1t_all = [R4[:, 512 * i:512 * (i + 1)] for i in range(8)]
    T1_t_all = [Tile("T1t%d" % i) for i in range(8)]
    T1_t = T1_t_all
    handover(list(RT.values()), T1_t)
    ws6_t = [Tile("ws6_%d" % i) for i in range(4)]
    handover([ws_t[0], ws_t[1], ws_t[2], Ss_t, Vbd_t, gs3_t[1], gs3_t[2]] + list(SAT[1].values()), ws6_t)

    def load_t1(f):
        s6 = (f - 1) % 4
        views = []
        for wi, (src, c0) in enumerate(((w_rb, 128 * f), (w_in, 7680 + 128 * f), (w_in, 8704 + 128 * f))):
            v = WSb[:, s6 * 3072 + wi * 1024:s6 * 3072 + (wi + 1) * 1024].rearrange("p (k n) -> p k n", k=8)
            pool.dma(v, src[0:1024, c0:c0 + 128].rearrange("(k p) n -> p k n", p=128), "ws6_%d" % s6, writes=[ws6_t[s6]],
                     chain=(wi > 0))
            views.append(v)
        return views
    t1w = {0: t1s0}
    for f_ in range(1, 4):
        t1w[f_] = load_t1(f_)
    it = 0
    for f in range(8):
        if f >= 1 and f + 3 < 8:
            t1w[f + 3] = load_t1(f + 3)
        v_rb, v_gr, v_ga = t1w[f]
        w6 = t1s0_t if f == 0 else ws6_t[(f - 1) % 4]
        if True:
            fs = slice(0, 128)
            for (t0, w) in TB:
                bo_ = 4 * (it % 2)
                T1t = T1t_all[4 * (it % 2):4 * (it % 2) + 4]
                T1_t = T1_t_all[4 * (it % 2):4 * (it % 2) + 4]
                it += 1
                ts = slice(t0, t0 + w)
                for k in range(8):
                    pe.op(lambda e, k=k: e.matmul(ps[:, bo_, 0:w], lhsT=v_rb[:, k, fs], rhs=retT[:, k, ts], start=(k == 0), stop=(k == 7)),
                          reads=[w6, retT_t], writes=[bank_t[bo_]], signal=(k == 7))
                for k in range(4):
                    pe.op(lambda e, k=k: e.matmul(ps[:, bo_ + 1, 0:w], lhsT=wab[:, k, 128 * f:128 * (f + 1)], rhs=attT[:, k, ts], start=(k == 0), stop=(k == 3)),
                          reads=[wab_t, attT_t], writes=[bank_t[bo_ + 1]], signal=(k == 3))
                for k in range(8):
                    pe.op(lambda e, k=k: e.matmul(ps[:, bo_ + 2, 0:w], lhsT=v_gr[:, k, fs], rhs=hT[:, k, ts], start=(k == 0), stop=(k == 7)),
                          reads=[w6, hT_t], writes=[bank_t[bo_ + 2]], signal=(k == 7))
                for k in range(8):
                    pe.op(lambda e, k=k: e.matmul(ps[:, bo_ + 3, 0:w], lhsT=v_ga[:, k, fs], rhs=hT[:, k, ts], start=(k == 0), stop=(k == 7)),
                          reads=[w6, hT_t], writes=[bank_t[bo_ + 3]], signal=(k == 7))
                act.op(lambda e: e.activation(out=T1t[0][:, 0:w], in_=ps[:, bo_ + 2, 0:w], func=AF.Sigmoid),
                       reads=[bank_t[bo_ + 2]], writes=[T1_t[0]])
                act.op(lambda e: e.activation(out=T1t[1][:, 0:w], in_=ps[:, bo_ + 3, 0:w], func=AF.Sigmoid),
                       reads=[bank_t[bo_ + 3]], writes=[T1_t[1]])
                dve.op(lambda e: e.tensor_tensor(out=T1t[2][:, 0:w], in0=ps[:, bo_, 0:w], in1=T1t[0][:, 0:w], op=ALU.mult),
                       reads=[bank_t[bo_], T1_t[0]], writes=[T1_t[2]])
                dve.op(lambda e: e.tensor_tensor(out=T1t[3][:, 0:w], in0=ps[:, bo_ + 1, 0:w], in1=T1t[1][:, 0:w], op=ALU.mult),
                       reads=[bank_t[bo_ + 1], T1_t[1]], writes=[T1_t[3]])
                dve.op(lambda e: e.tensor_tensor(out=mixedT[:, f, ts], in0=T1t[2][:, 0:w], in1=T1t[3][:, 0:w], op=ALU.add),
                       reads=[T1_t[2], T1_t[3]], writes=[mixed_t])

    xlo = R2[:, :].rearrange("p (k n) -> p k n", k=4)
    xhi = R4[:, :].rearrange("p (k n) -> p k n", k=4)

    def xacc(f):
        return (xlo if f < 4 else xhi)[:, f % 4, :]
    xa_t = [Tile("xa%d" % f) for f in range(8)]
    handover([retT_t], xa_t[0:4])
    handover(T1_t_all, xa_t[4:8])
    scr = [R5f[:, 8208:9232], R5f[:, 9232:10256]]
    scr_t = [Tile("scr0"), Tile("scr1")]
    sqbs = [R5[:, 22560:23072], R5[:, 23584:24096]]
    sqb_ts = [Tile("sqb0"), Tile("sqb1")]
    handover([wab_t, t1s0_t], scr_t + sqb_ts)
    junk2 = R5[:, 22560:23584]
    scr4 = scr + [R5f[:, 10256:11280]]
    scr4_t = scr_t + [Tile("scr2")]
    handover([wab_t, t1s0_t], [scr4_t[2]])
    for t in range(17):
        rows = 128 if t < 16 else 4
        sl = t % 3
        tok = slice(t * 128, t * 128 + rows)
        src = x[t * 128:(t + 1) * 128, :] if t < 16 else xs[:, :]
        sp.dma(scr4[sl][0:rows, :], src, "scr%d" % sl, writes=[scr4_t[sl]])
        bp = 2 * (t % 4)
        for k in range(8):
            pe.op(lambda e, k=k: e.transpose(out=ps[:, bp + k // 4, 128 * (k % 4):128 * (k % 4) + rows],
                                             in_=scr4[sl][0:rows, 128 * k:128 * (k + 1)], identity=identf[0:rows, 0:rows]),
                  reads=[scr4_t[sl], cf_t], writes=[bank_t[bp], bank_t[bp + 1]], signal=(k == 7))
        for half, xv in ((0, xlo), (1, xhi)):
            act.op(lambda e, half=half, xv=xv: e.copy(out=xv[:, :, tok],
                                                      in_=ps[:, bp + half, :].rearrange("p (k c) -> p k c", k=4)[:, :, 0:rows]),
                   reads=[bank_t[bp + half]], writes=xa_t[4 * half:4 * half + 4])
    handover(ws6_t, [ws_t[0], ws_t[1], ws_t[2]])
    s_o0, v_o0 = load_w(w_o, 0, 0, 512, 8, slot=0)
    s_o1, v_o1 = load_w(w_o, 0, 512, 512, 8, slot=1)
    mlp_pre = load_w(w_up, 0, 0, 512, 8, slot=2)
    v_o = (v_o0, v_o1)
    h2T = hT
    h2_t = Tile("h2T")
    handover([hT_t], [h2_t])
    R3f = R3[:, :].bitcast(F32)
    rstd = R3f[:, 0:NT]
    rstd_t = Tile("rstd")
    handover([attT_t], [rstd_t])
    g2c = sm[:, 120:128]
    g2_t = Tile("g2c")
    sp.dma(sm[0:8, 128:256], ln2g.rearrange("(f p) -> f p", p=128), "c_g2", writes=[g2_t])
    pe.op(lambda e: e.transpose(out=ps[:, 7, 0:8], in_=sm[0:8, 128:256], identity=identf[0:8, 0:8]),
          reads=[g2_t, cf_t], writes=[bank_t[7]])
    act.op(lambda e: e.copy(out=g2c, in_=ps[:, 7, 0:8]), reads=[bank_t[7]], writes=[g2_t])
    def h2_scale(f, tsl):
        dve.op(lambda e: e.scalar_tensor_tensor(out=h2T[:, f, tsl], in0=xacc(f)[:, tsl], scalar=g2c[:, f:f + 1],
                                                in1=rstd[:, tsl], op0=ALU.mult, op1=ALU.mult),
               reads=[xa_t[f], g2_t, rstd_t], writes=[h2_t])
    it = 0
    prev_tb = None
    for (t0, w) in TB:
        ts = slice(t0, t0 + w)
        for f in range(8):
            bo_ = it % 4
            it += 1
            if prev_tb is not None:
                h2_scale(f, prev_tb)
            for k in range(8):
                pe.op(lambda e, k=k: e.matmul(ps[:, bo_, 0:w], lhsT=v_o[f // 4][:, k, 128 * (f % 4):128 * (f % 4 + 1)],
                                              rhs=mixedT[:, k, ts], start=(k == 0), stop=(k == 7)),
                      reads=[ws_t[f // 4], mixed_t], writes=[bank_t[bo_]], signal=(k == 7))
            dve.op(lambda e: e.tensor_tensor(out=xacc(f)[:, ts], in0=ps[:, bo_, 0:w], in1=xacc(f)[:, ts], op=ALU.add),
                   reads=[bank_t[bo_], xa_t[f]], writes=[xa_t[f]])
            sqb, sqb_t = sqbs[f % 2], sqb_ts[f % 2]
            act.op(lambda e: e.activation(out=sqb[:, 0:w], in_=xacc(f)[:, ts], func=AF.Square),
                   reads=[xa_t[f]], writes=[sqb_t])
            if f > 0:
                pe.op(lambda e, f=f: e.matmul(ps[:, 6, 0:w], lhsT=onesb, rhs=sqbs[(f - 1) % 2][:, 0:w], start=(f == 1), stop=False),
                      reads=[sqb_ts[(f - 1) % 2], cb_t], writes=[bank_t[6]], signal=True)
        pe.op(lambda e: e.matmul(ps[:, 6, 0:w], lhsT=onesb, rhs=sqbs[1][:, 0:w], start=False, stop=True),
              reads=[sqb_ts[1], cb_t], writes=[bank_t[6]], signal=True)
        act.op(lambda e: e.activation(out=rstd[:, ts], in_=ps[:, 6, 0:w], func=AF.Ln, bias=epsc, scale=1.0 / D),
               reads=[bank_t[6], cf_t], writes=[rstd_t])
        act.op(lambda e: e.activation(out=rstd[:, ts], in_=rstd[:, ts], func=AF.Exp, scale=-0.5), reads=[rstd_t], writes=[rstd_t])
        prev_tb = ts
    for f in range(8):
        h2_scale(f, prev_tb)

    pleT = R3[:, 2 * NT:4 * NT].rearrange("p (k n) -> p k n", k=2)
    ple_t = Tile("pleT")
    handover([attT_t], [ple_t])
    ppb = R5[:, 20512:22560]
    ppb_t = Tile("ppb")
    handover([scr4_t[2]], [ppb_t])
    for hf in range(3):
        if hf < 2:
            pool.dma(ppb.rearrange("p (t c) -> p t c", t=8),
                     pp[1024 * hf:1024 * (hf + 1), :].rearrange("(t p) c -> p t c", p=128), "ppb", writes=[ppb_t])
            tl = [(8 * hf + i, i, 128) for i in range(8)]
        else:
            pool.dma(ppb[0:4, 0:256], psm[:, :], "ppb", writes=[ppb_t])
            tl = [(16, 0, 4)]
        for (t, i, rows) in tl:
            for cc in range(2):
                pe.op(lambda e, i=i, cc=cc, rows=rows: e.transpose(
                    out=bankbf(6 + i // 4)[:, 256 * (i % 4) + 128 * cc:256 * (i % 4) + 128 * cc + rows],
                    in_=ppb[0:rows, 256 * i + 128 * cc:256 * i + 128 * (cc + 1)], identity=identb[0:rows, 0:rows]),
                    reads=[ppb_t, cb_t], writes=[bank_t[6 + i // 4]], signal=(cc == 1))
            act.op(lambda e, t=t, i=i, rows=rows: e.copy(
                out=pleT[:, :, t * 128:t * 128 + rows],
                in_=bankbf(6 + i // 4)[:, 256 * (i % 4):256 * (i % 4 + 1)].rearrange("p (c n) -> p c n", c=2)[:, :, 0:rows]),
                reads=[bank_t[6 + i // 4]], writes=[ple_t])
    wpl_t = Tile("wpl")
    handover([rstd_t], [wpl_t])
    v_pl = R3[:, 0:2048].rearrange("p (k n) -> p k n", k=2)
    pool.dma(v_pl, w_pl[:, :].rearrange("(k p) n -> p k n", p=128), "wpl", writes=[wpl_t])
    rTb = [R5[:, 8208 * i:8208 * (i + 1)].rearrange("p (k n) -> p k n", k=4) for i in range(2)]
    rT_t = [Tile("rT0"), Tile("rT1")]
    handover([mixed_t], rT_t)
    it = 0
    for j in range(8):
        if j == 0:
            su, vu = mlp_pre
            ws_state["n"] = 0
        else:
            su, vu = load_w(w_up, 0, 512 * j, 512, 8)
        sd_, vd_ = load_w(w_dn, 512 * j, 0, 1024, 4)
        if j == 7:
            free_slot = 3 - su - sd_
            s_g0, v_g0 = load_w(w_pg, 0, 0, 512, 8, slot=free_slot)
        issue_d2d(j // 4, j % 4)
        W_ = WINS[2]
        dst_, src_ = ((ks[2], ck[2]), (vs[2], cv[2]))[j % 2]
        act.dma(dst_[j // 2, 0:W_ - 1, :].rearrange("w c -> (w c)").rearrange("(a n) -> a n", a=16),
                src_[j // 2, 1:W_, :].rearrange("w c -> (w c)").rearrange("(a n) -> a n", a=16), "d2d")
        rb = j % 2
        for fi in range(4):
            for (t0, w) in TB:
                ts = slice(t0, t0 + w)
                bo_ = it % 4
                it += 1
                sl = it % 2
                for k in range(8):
                    pe.op(lambda e, k=k: e.matmul(ps[:, bo_, 0:w], lhsT=vu[:, k, 128 * fi:128 * (fi + 1)], rhs=h2T[:, k, ts],
                                                  start=(k == 0), stop=(k == 7)),
                          reads=[ws_t[su], h2_t], writes=[bank_t[bo_]], signal=(k == 7))
                act.op(lambda e: e.activation(out=scr[sl][:, 0:w], in_=ps[:, bo_, 0:w], func=AF.Relu),
                       reads=[bank_t[bo_]], writes=[scr_t[sl]])
                dve.op(lambda e: e.tensor_tensor(out=rTb[rb][:, fi, ts], in0=ps[:, bo_, 0:w], in1=scr[sl][:, 0:w], op=ALU.mult),
                       reads=[bank_t[bo_], scr_t[sl]], writes=[rT_t[rb]])
        if j == 7:
            s_g1, v_g1 = load_w(w_pg, 0, 512, 512, 8, slot=su)
        for f in range(8):
            for (t0, w) in TB:
                ts = slice(t0, t0 + w)
                bo_ = 4 + it % 4
                it += 1
                for k in range(4):
                    pe.op(lambda e, k=k: e.matmul(ps[:, bo_, 0:w], lhsT=vd_[:, k, 128 * f:128 * (f + 1)], rhs=rTb[rb][:, k, ts],
                                                  start=(k == 0), stop=(k == 3)),
                          reads=[ws_t[sd_], rT_t[rb]], writes=[bank_t[bo_]], signal=(k == 3))
                dve.op(lambda e: e.tensor_tensor(out=xacc(f)[:, ts], in0=ps[:, bo_, 0:w], in1=xacc(f)[:, ts], op=ALU.add),
                       reads=[bank_t[bo_], xa_t[f]], writes=[xa_t[f]])

    x2b = hT
    x2b_t = Tile("x2b")
    handover([h2_t], [x2b_t])
    for f in range(8):
        act.op(lambda e, f=f: e.copy(out=x2b[:, f, :], in_=xacc(f)), reads=[xa_t[f]], writes=[x2b_t])
    v_g = (v_g0, v_g1)
    s_g = (s_g0, s_g1)
    it = 0
    for (t0, w) in TB:
        ts = slice(t0, t0 + w)
        for f in range(8):
            bo_ = 2 * (it % 3)
            it += 1
            sl = it % 2
            for k in range(8):
                pe.op(lambda e, k=k: e.matmul(ps[:, bo_, 0:w], lhsT=v_g[f // 4][:, k, 128 * (f % 4):128 * (f % 4 + 1)],
                                              rhs=x2b[:, k, ts], start=(k == 0), stop=(k == 7)),
                      reads=[ws_t[s_g[f // 4]], x2b_t], writes=[bank_t[bo_]], signal=(k == 7))
            for k in range(2):
                pe.op(lambda e, k=k: e.matmul(ps[:, bo_ + 1, 0:w], lhsT=v_pl[:, k, 128 * f:128 * (f + 1)], rhs=pleT[:, k, ts],
                                              start=(k == 0), stop=(k == 1)),
                      reads=[wpl_t, ple_t], writes=[bank_t[bo_ + 1]], signal=(k == 1))
            act.op(lambda e: e.activation(out=scr[sl][:, 0:w], in_=ps[:, bo_, 0:w], func=AF.Sigmoid),
                   reads=[bank_t[bo_]], writes=[scr_t[sl]])
            dve.op(lambda e: e.tensor_tensor(out=scr[sl][:, 0:w], in0=ps[:, bo_ + 1, 0:w], in1=scr[sl][:, 0:w], op=ALU.mult),
                   reads=[bank_t[bo_ + 1], scr_t[sl]], writes=[scr_t[sl]])
            dve.op(lambda e: e.tensor_tensor(out=xacc(f)[:, ts], in0=xacc(f)[:, ts], in1=scr[sl][:, 0:w], op=ALU.add),
                   reads=[xa_t[f], scr_t[sl]], writes=[xa_t[f]])

    sp.dma(gt[:, :], lnfg[0:1, :].broadcast_to([128, D]), "c_gt", writes=[gt_t])
    junk2_t = Tile("junk2")
    st5_t = [Tile("st5a"), Tile("st5b")]
    for t in range(17):
        rows = 128 if t < 16 else 4
        tok = slice(t * 128, t * 128 + rows)
        bp = 2 * (t % 4)
        sl = t % 2
        for f in range(8):
            pe.op(lambda e, f=f: e.transpose(out=ps[0:rows, bp + f // 4, 128 * (f % 4):128 * (f % 4 + 1)], in_=xacc(f)[:, tok],
                                             identity=identf),
                  reads=[xa_t[f], cf_t], writes=[bank_t[bp], bank_t[bp + 1]], signal=(f == 7))
        yv = ps[0:rows, bp:bp + 2, :].rearrange("p a n -> p (a n)")
        c5 = 100 + 4 * sl
        st5 = st5_t[sl]
        act.op(lambda e: e.activation(out=junk2[0:rows, :], in_=yv, func=AF.Square, accum_out=sm[0:rows, c5:c5 + 1]),
               reads=[bank_t[bp], bank_t[bp + 1]], writes=[junk2_t, st5])
        act.op(lambda e: e.activation(out=sm[0:rows, c5 + 1:c5 + 2], in_=sm[0:rows, c5:c5 + 1], func=AF.Sqrt, bias=epsc[0:rows, :],
                                      scale=1.0 / D), reads=[st5, cf_t], writes=[st5])
        dve.op(lambda e: e.reciprocal(out=sm[0:rows, c5 + 2:c5 + 3], in_=sm[0:rows, c5 + 1:c5 + 2]), reads=[st5], writes=[st5])
        dve.op(lambda e: e.scalar_tensor_tensor(out=scr[sl][0:rows, :], in0=yv, scalar=sm[0:rows, c5 + 2:c5 + 3], in1=gt[0:rows, :],
                                                op0=ALU.mult, op1=ALU.mult),
               reads=[bank_t[bp], bank_t[bp + 1], st5, gt_t], writes=[scr_t[sl]])
        dst = y[t * 128:(t + 1) * 128, :] if t < 16 else ys[:, :]
        sp.dma(dst, scr[sl][0:rows, :], "scr%d" % sl, reads=[scr_t[sl]])

    return nc, P, sp, locals()


def finish(nc, P, sp):
    for k, v in P.semval.items():
        if v > 0 and not k.startswith("e_"):
            if sp.known.get(k, 0) < v:
                sp.h.wait_ge(P.sems[k], v)
                sp.known[k] = v


def _rel_bucket(dist):
    d = dist.astype(np.int32)
    lr = np.log(np.maximum(d, 1).astype(np.float32) / np.float32(16)) / np.float32(math.log(2048 / 16))
    large = 16 + (lr * np.float32(16)).astype(np.int32)
    large = np.minimum(large, 31)
    return np.where(d < 16, d, large)


def make_consts():
    cfm = np.zeros((128, NCF), np.float32)

    def put(name, arr):
        o, w = CF[name]
        cfm[:, o:o + w] = arr

    put("identf", np.eye(128, dtype=np.float32))
    put("onesf", np.ones((128, 128), np.float32))
    inv = (np.float32(10000.0) ** (-(np.arange(32, dtype=np.float32)) / np.float32(32))).astype(np.float32)
    pos = np.zeros((128, 17), np.float32)
    for c in range(16):
        pos[:, c] = c * 128 + np.arange(128)
    pos[:, 16] = 16384.0
    ang = (pos[:, :, None] * inv[None, None, :]).astype(np.float32)
    put("cos", np.cos(ang).astype(np.float32).reshape(128, 17 * 32))
    put("sin", np.sin(ang).astype(np.float32).reshape(128, 17 * 32))
    lg = np.log1p(-np.exp2(-5.0 - np.arange(8, dtype=np.float64)))
    p1 = (np.arange(128, dtype=np.float64) + 1.0)[:, None]
    put("qdec", np.exp(p1 * lg[None, :]))
    put("kdec", np.exp(-p1 * lg[None, :]) / 8.0)
    put("qdec_s", np.tile(np.exp(lg)[None, :], (128, 1)))
    put("kdec_s", np.full((128, 8), 1.0 / 8.0))
    gam = np.zeros((128, 4))
    for p in range(128):
        for pr in range(4):
            gam[p, pr] = np.exp(128.0 * lg[2 * pr + p // 64])
    put("gamC", gam)
    bd = np.zeros((128, 4))
    bd[0:4, 0:4] = np.eye(4)
    put("bd", bd)
    sel = np.zeros((128, 512))
    for b in range(4):
        sel[b, 128 * b:128 * (b + 1)] = 1.0
    put("sel", sel)
    put("eps", np.full((128, 1), EPS))
    cbm = np.zeros((128, NCB), np.float32)
    cbm[:, 0:128] = np.eye(128)
    cbm[:, 128:256] = 1.0
    jj = np.arange(128)
    cbm[:, 256:384] = (jj[:, None] <= jj[None, :]).astype(np.float32)
    ohg = np.zeros((33, 3 * 383), np.float32)
    ohs = np.zeros((33, 3 * 128), np.float32)
    for g in range(3):
        dil = DILS[g]
        bk = _rel_bucket(np.arange(128) * dil)
        ohg[32, 383 * g:383 * (g + 1)] = NEG
        for j in range(128):
            ohg[bk[j], 383 * g + 127 + j] = 1.0
            ohg[32, 383 * g + 127 + j] = 0.0
        for m in range(128):
            j = 0 if m == 0 else 128 - m
            ohs[bk[j], 128 * g + m] = 1.0
    return cfm, cbm.astype(ml_dtypes.bfloat16), ohg, ohs


def make_in_maps(inp, cores):
    cfm, cbm, ohg, ohs = make_consts()
    f = lambda a: np.ascontiguousarray(np.asarray(a, dtype=np.float32))
    shared = {
        "ln1g": f(inp["ln1_g"]), "w_in": f(inp["w_in"][0]), "gng": f(inp["ret_gn_g"]), "w_rb": f(inp["w_ret_br"][0]),
        "w_ab": f(inp["w_att_br"][0]), "w_o": f(inp["w_out"][0]), "ln2g": f(inp["ln2_g"][0]), "w_up": f(inp["w_up"][0]),
        "w_dn": f(inp["w_down"][0]), "w_pl": f(inp["w_ple"][0]), "w_pg": f(inp["w_ple_gate"][0]),
        "relb": f(inp["rel_bias"]), "lnfg": f(inp["lnf_g"]).reshape(1, D), "cf": cfm, "cb": cbm, "ohg": ohg, "ohs": ohs,
    }
    maps = []
    for c in cores:
        m = dict(shared)
        m["x"] = f(inp["x_prompt"][c])
        m["xs"] = f(inp["x_sample"][4 * c:4 * c + 4, 0])
        m["st"] = f(inp["state_ret"][0, 4 * c:4 * c + 4])
        caches_k = (inp["cache_k_w128"], inp["cache_k_w512"], inp["cache_k_w2048"])
        caches_v = (inp["cache_v_w128"], inp["cache_v_w512"], inp["cache_v_w2048"])
        for g, w in enumerate(WINS):
            m["ck%d" % g] = f(caches_k[g][0, 4 * c:4 * c + 4]).reshape(4, w, 512)
            m["cv%d" % g] = f(caches_v[g][0, 4 * c:4 * c + 4]).reshape(4, w, 512)
        m["pp"] = f(inp["p_prompt"][0, c])
        m["psm"] = f(inp["p_sample"][0, 4 * c:4 * c + 4, 0])
        maps.append(m)
    return maps


_CACHE = {}


def kernel(**inputs):
    if "nc" not in _CACHE:
        nc, P, sp, _ = build_program()
        finish(nc, P, sp)
        _CACHE["nc"] = nc
    nc = _CACHE["nc"]
    maps = make_in_maps(inputs, list(range(8)))
    res = run_bass_kernel_spmd(nc, maps, core_ids=list(range(8)))
    R = res.results
    f32 = np.float32
    cat = lambda name: np.stack([np.asarray(R[c][name]).astype(f32) for c in range(8)], 0)
    y = cat("y")
    ys = np.concatenate([np.asarray(R[c]["ys"]).astype(f32) for c in range(8)], 0).reshape(32, 1, D)
    outs = [y, ys, cat("stp")[None]]
    for g, w in enumerate(WINS):
        outs.append(cat("kp%d" % g).reshape(1, 8, w, 4, 128))
        outs.append(cat("vp%d" % g).reshape(1, 8, w, 4, 128))
    outs.append(np.concatenate([np.asarray(R[c]["sts"]).astype(f32) for c in range(8)], 0)[None])
    for g, w in enumerate(WINS):
        outs.append(np.concatenate([np.asarray(R[c]["ks%d" % g]).astype(f32) for c in range(8)], 0).reshape(1, 32, w, 4, 128))
        outs.append(np.concatenate([np.asarray(R[c]["vs%d" % g]).astype(f32) for c in range(8)], 0).reshape(1, 32, w, 4, 128))
    return tuple(outs)
```
